# Optimizing a Trainium2 kernel written in Bass

```python
import jax, jax.numpy as jnp
from jax import lax
import numpy as np

D_MODEL = 2048
BATCH = 4
SEQ = 2048
DEPTH = 1
DEC_BATCH = 128
DEC_SEQ = 8
PAST_LEN = 16384
PAGE_SIZE = 128

LRU_WIDTH = D_MODEL // 2
SC_WIDTH = D_MODEL - LRU_WIDTH
LRU_HEADS = 16
LRU_HEAD_DIM = LRU_WIDTH // LRU_HEADS
LRU_CONV = 4
LRU_C = 8.0
SC_GROUPS = 16
SC_CONV = 3
IN_COLS = 2 * LRU_WIDTH + 3 * SC_WIDTH
PEER_HEADS = 8
N_KEYS = 128
N_EXPERTS = N_KEYS * N_KEYS
PEER_TOPK = 16
QUERY_DIM = 256
HALF_DIM = QUERY_DIM // 2
PEER_BLOCK = 128
EPS = 1e-6

kernel_name = "hymba_rglru_shortconv_peer_adaln_step"


def rmsnorm(x, g):
    xf = x.astype(jnp.float32)
    y = xf * lax.rsqrt(jnp.mean(xf * xf, axis=-1, keepdims=True) + EPS)
    return (y * g.astype(jnp.float32)).astype(x.dtype)


def modulate(h, shift, scale):
    return h * (1 + scale[:, None, :]) + shift[:, None, :]


def causal_dwconv(x, buf, w):
    width = w.shape[0]
    s = x.shape[1]
    xp = jnp.concatenate([buf.astype(x.dtype), x], axis=1)
    out = w[0] * xp[:, 0:s]
    for k in range(1, width):
        out = out + w[k] * xp[:, k:k + s]
    return out, xp[:, -(width - 1):]


def rglru(x, h0, w_a, b_a, w_x, b_x, lam):
    b, s, _ = x.shape
    xh = x.reshape(b, s, LRU_HEADS, LRU_HEAD_DIM)
    r = jax.nn.sigmoid((jnp.einsum('bshi,hij->bshj', xh, w_a).reshape(b, s, LRU_WIDTH) + b_a).astype(jnp.float32))
    i = jax.nn.sigmoid((jnp.einsum('bshi,hij->bshj', xh, w_x).reshape(b, s, LRU_WIDTH) + b_x).astype(jnp.float32))
    log_a = -LRU_C * r * jax.nn.softplus(-lam.astype(jnp.float32))
    a = jnp.exp(log_a)
    u = jnp.sqrt(-jnp.expm1(2.0 * log_a)) * (i * x.astype(jnp.float32))

    def step(h, au):
        a_t, u_t = au
        h = a_t * h + u_t
        return h, h

    h_last, hs = lax.scan(step, h0.astype(jnp.float32),
                          (jnp.swapaxes(a, 0, 1), jnp.swapaxes(u, 0, 1)))
    return jnp.swapaxes(hs, 0, 1).astype(x.dtype), h_last.astype(x.dtype)


def peer(h, w_q, sub_keys, expert_u, expert_v):
    b, s, d = h.shape
    t = h.reshape(-1, d)
    n_tok = t.shape[0]
    nb = -(-n_tok // PEER_BLOCK)
    t = jnp.pad(t, ((0, nb * PEER_BLOCK - n_tok), (0, 0)))

    def block(xb):
        q = (xb @ w_q).reshape(PEER_BLOCK, PEER_HEADS, 2, HALF_DIM)
        sc = jnp.einsum('thpk,pnk->thpn', q, sub_keys).astype(jnp.float32)
        v1, i1 = lax.top_k(sc[:, :, 0], PEER_TOPK)
        v2, i2 = lax.top_k(sc[:, :, 1], PEER_TOPK)
        cand = (v1[..., :, None] + v2[..., None, :]).reshape(PEER_BLOCK, PEER_HEADS, PEER_TOPK * PEER_TOPK)
        cidx = (i1[..., :, None] * N_KEYS + i2[..., None, :]).reshape(PEER_BLOCK, PEER_HEADS, PEER_TOPK * PEER_TOPK)
        top, pos = lax.top_k(cand, PEER_TOPK)
        eidx = jnp.take_along_axis(cidx, pos, axis=-1)
        g = jax.nn.softmax(top, axis=-1).astype(xb.dtype)
        act = jax.nn.gelu(jnp.einsum('td,thkd->thk', xb, expert_u[eidx])) * g
        return jnp.einsum('thk,thkd->td', act, expert_v[eidx])

    out = lax.map(block, t.reshape(nb, PEER_BLOCK, d)).reshape(-1, d)[:n_tok]
    return out.reshape(b, s, d)


def layer(x, c, h0, lru_buf, sc_buf, w_ada, b_ada, norm1_g, norm2_g, w_in,
          lru_conv_w, lru_conv_b, lru_wa, lru_ba, lru_wx, lru_bx, lru_lambda,
          sconv_w, gnorm_lru_g, gnorm_sc_g, w_out, peer_wq, peer_sub_keys, peer_u, peer_v):
    mod = (jax.nn.silu(c) @ w_ada + b_ada).reshape(c.shape[0], 6, D_MODEL)
    shift1, scale1, gate1, shift2, scale2, gate2 = [mod[:, k] for k in range(6)]

    h = modulate(rmsnorm(x, norm1_g), shift1, scale1)
    proj = h @ w_in
    x_lru, y_gate, sc_b, sc_c, sc_x = jnp.split(
        proj, [LRU_WIDTH, 2 * LRU_WIDTH, 2 * LRU_WIDTH + SC_WIDTH, 2 * LRU_WIDTH + 2 * SC_WIDTH], axis=-1)

    xc, new_lru_buf = causal_dwconv(x_lru, lru_buf, lru_conv_w)
    rec, h_last = rglru(xc + lru_conv_b, h0, lru_wa, lru_ba, lru_wx, lru_bx, lru_lambda)
    out_lru = rec * jax.nn.gelu(y_gate)

    conv_out, new_sc_buf = causal_dwconv(sc_c * sc_x, sc_buf, sconv_w)
    out_sc = sc_b * conv_out

    mix = jnp.concatenate([rmsnorm(out_lru, gnorm_lru_g), rmsnorm(out_sc, gnorm_sc_g)], axis=-1) @ w_out
    x = x + gate1[:, None, :] * mix

    h2 = modulate(rmsnorm(x, norm2_g), shift2, scale2)
    x = x + gate2[:, None, :] * peer(h2, peer_wq, peer_sub_keys, peer_u, peer_v)
    return x, h_last, new_lru_buf, new_sc_buf


def setup_inputs(seed: int = 0) -> dict:
    key = jax.random.key(seed)
    ks = jax.random.split(key, 32)
    f32 = jnp.float32
    nrm = lambda k, shape, s: jax.random.normal(k, shape, f32) * s
    d = D_MODEL
    a0 = jax.random.uniform(ks[20], (DEPTH, LRU_WIDTH), f32, 0.9, 0.999)
    sig = a0 ** (1.0 / LRU_C)
    return {
        "x_prompt": nrm(ks[0], (BATCH, SEQ, d), 1.0),
        "x_sample": nrm(ks[1], (DEC_BATCH, DEC_SEQ, d), 1.0),
        "c_prompt": nrm(ks[2], (BATCH, d), 1.0),
        "c_sample": nrm(ks[3], (DEC_BATCH, d), 1.0),
        "state_lru_h": nrm(ks[4], (DEPTH, DEC_BATCH, LRU_WIDTH), 0.5),
        "state_lru_conv": nrm(ks[5], (DEPTH, DEC_BATCH, LRU_CONV - 1, LRU_WIDTH), 1.0),
        "state_sconv": nrm(ks[6], (DEPTH, DEC_BATCH, SC_CONV - 1, SC_WIDTH), 1.0),
        "w_ada": nrm(ks[7], (DEPTH, d, 6 * d), 0.3 * d ** -0.5),
        "b_ada": nrm(ks[8], (DEPTH, 6 * d), 0.01),
        "norm1_g": 1.0 + nrm(ks[9], (DEPTH, d), 0.02),
        "norm2_g": 1.0 + nrm(ks[10], (DEPTH, d), 0.02),
        "w_in": nrm(ks[11], (DEPTH, d, IN_COLS), d ** -0.5),
        "lru_conv_w": nrm(ks[12], (DEPTH, LRU_CONV, LRU_WIDTH), LRU_CONV ** -0.5),
        "lru_conv_b": nrm(ks[13], (DEPTH, LRU_WIDTH), 0.01),
        "lru_wa": nrm(ks[14], (DEPTH, LRU_HEADS, LRU_HEAD_DIM, LRU_HEAD_DIM), LRU_HEAD_DIM ** -0.5),
        "lru_ba": nrm(ks[15], (DEPTH, LRU_WIDTH), 0.01),
        "lru_wx": nrm(ks[16], (DEPTH, LRU_HEADS, LRU_HEAD_DIM, LRU_HEAD_DIM), LRU_HEAD_DIM ** -0.5),
        "lru_bx": nrm(ks[17], (DEPTH, LRU_WIDTH), 0.01),
        "lru_lambda": jnp.log(sig / (1.0 - sig)),
        "sconv_w": nrm(ks[18], (DEPTH, SC_CONV, SC_WIDTH), SC_CONV ** -0.5),
        "gnorm_lru_g": 1.0 + nrm(ks[19], (DEPTH, LRU_WIDTH), 0.02),
        "gnorm_sc_g": 1.0 + nrm(ks[21], (DEPTH, SC_WIDTH), 0.02),
        "w_out": nrm(ks[22], (DEPTH, d, d), d ** -0.5),
        "peer_wq": nrm(ks[23], (DEPTH, d, PEER_HEADS * QUERY_DIM), d ** -0.5),
        "peer_sub_keys": nrm(ks[24], (DEPTH, 2, N_KEYS, HALF_DIM), HALF_DIM ** -0.5),
        "peer_u": nrm(ks[25], (DEPTH, N_EXPERTS, d), d ** -0.5),
        "peer_v": nrm(ks[26], (DEPTH, N_EXPERTS, d), 0.3),
        "final_g": 1.0 + nrm(ks[27], (d,), 0.02),
    }


def reference(x_prompt, x_sample, c_prompt, c_sample, state_lru_h, state_lru_conv, state_sconv,
              w_ada, b_ada, norm1_g, norm2_g, w_in, lru_conv_w, lru_conv_b, lru_wa, lru_ba,
              lru_wx, lru_bx, lru_lambda, sconv_w, gnorm_lru_g, gnorm_sc_g, w_out,
              peer_wq, peer_sub_keys, peer_u, peer_v, final_g):
    xp, xs = x_prompt, x_sample
    bp = x_prompt.shape[0]
    zero_h = jnp.zeros((bp, LRU_WIDTH), x_prompt.dtype)
    zero_lbuf = jnp.zeros((bp, LRU_CONV - 1, LRU_WIDTH), x_prompt.dtype)
    zero_sbuf = jnp.zeros((bp, SC_CONV - 1, SC_WIDTH), x_prompt.dtype)
    hp_l, lbp_l, sbp_l, hs_l, lbs_l, sbs_l = [], [], [], [], [], []
    for l in range(DEPTH):
        lp = (w_ada[l], b_ada[l], norm1_g[l], norm2_g[l], w_in[l], lru_conv_w[l], lru_conv_b[l],
              lru_wa[l], lru_ba[l], lru_wx[l], lru_bx[l], lru_lambda[l], sconv_w[l],
              gnorm_lru_g[l], gnorm_sc_g[l], w_out[l], peer_wq[l], peer_sub_keys[l], peer_u[l], peer_v[l])
        xp, hp, lbp, sbp = layer(xp, c_prompt, zero_h, zero_lbuf, zero_sbuf, *lp)
        xs, hs, lbs, sbs = layer(xs, c_sample, state_lru_h[l], state_lru_conv[l], state_sconv[l], *lp)
        hp_l.append(hp); lbp_l.append(lbp); sbp_l.append(sbp)
        hs_l.append(hs); lbs_l.append(lbs); sbs_l.append(sbs)
    y_prompt = rmsnorm(xp, final_g)
    y_sample = rmsnorm(xs, final_g)
    return (y_prompt, y_sample,
            jnp.stack(hp_l), jnp.stack(lbp_l), jnp.stack(sbp_l),
            jnp.stack(hs_l), jnp.stack(lbs_l), jnp.stack(sbs_l))
```

```python
import numpy as np
from contextlib import ExitStack
import concourse.bass as bass
import concourse.mybir as mybir
from concourse.bass_utils import run_bass_kernel_spmd

F32 = mybir.dt.float32
BF16 = mybir.dt.bfloat16
AF = mybir.ActivationFunctionType
ALU = mybir.AluOpType

DM = 2048
NMAIN = 1024
NSMP = 128
NTOK = NMAIN + NSMP
NPRE = 1024
NSEQ = 16
LS = 8
EPS = 1e-6
NEXP = 16384
GCH = 4
NGRP = 128 // GCH
NTILE = NTOK // 128
TT = 384


class Sched:
    ENGS = ("tensor", "vector", "scalar", "gpsimd", "sync")
    EPOCH = 30000

    def __init__(self, nc):
        self.nc = nc
        self.ops = []
        self.eng_count = {e: 0 for e in self.ENGS}
        self.last_writer = {}
        self.readers = {}
        self.dma_cum = {}
        self.waited = {e: {} for e in self.ENGS}
        self.semnames = []
        self.barrier_toks = None

    def _sem(self, name):
        if name not in self.semnames:
            self.semnames.append(name)

    def _deps(self, reads, writes):
        deps = set()
        for r in reads:
            w = self.last_writer.get(r)
            if w is not None:
                deps.add(w)
        for w_ in writes:
            w = self.last_writer.get(w_)
            if w is not None:
                deps.add(w)
            for rd in self.readers.get(w_, ()):
                deps.add(rd)
        if self.barrier_toks is not None:
            deps.update(self.barrier_toks)
        return deps

    def _record(self, tok, reads, writes):
        for r in reads:
            self.readers.setdefault(r, []).append(tok)
        for w_ in writes:
            self.last_writer[w_] = tok
            self.readers[w_] = []

    def _waits_for(self, eng, deps):
        waits = {}
        for (semname, val) in deps:
            if eng == "tensor" and semname.startswith("e_tensor"):
                continue
            if self.waited[eng].get(semname, 0) >= val:
                continue
            if waits.get(semname, 0) < val:
                waits[semname] = val
        for s, v in waits.items():
            self.waited[eng][s] = v
        return waits

    def op(self, eng, fn, reads=(), writes=()):
        deps = self._deps(reads, writes)
        waits = self._waits_for(eng, deps)
        n = self.eng_count[eng]
        self.eng_count[eng] = n + 1
        semname = "e_%s_%d" % (eng, n // self.EPOCH)
        self._sem(semname)
        tok = (semname, n % self.EPOCH + 1)
        self.ops.append((eng, fn, waits, (semname, 1)))
        self._record(tok, reads, writes)
        return tok

    def dma(self, eng, fn, key, reads=(), writes=()):
        deps = self._deps(reads, writes)
        waits = self._waits_for(eng, deps)
        semname = "d_" + key
        self._sem(semname)
        self.dma_cum[semname] = self.dma_cum.get(semname, 0) + 16
        tok = (semname, self.dma_cum[semname])
        self.ops.append((eng, fn, waits, (semname, 16)))
        self._record(tok, reads, writes)
        return tok

    def _all_toks(self):
        toks = set()
        for e in self.ENGS:
            n = self.eng_count[e]
            if n:
                toks.add(("e_%s_%d" % (e, (n - 1) // self.EPOCH), (n - 1) % self.EPOCH + 1))
        for s, v in self.dma_cum.items():
            toks.add((s, v))
        return toks

    def barrier(self):
        self.barrier_toks = self._all_toks()

    def emit(self, st):
        nc = self.nc
        sems = {}
        for s in self.semnames:
            sems[s] = st.enter_context(nc.semaphore(s))
        final = self._all_toks()
        block = st.enter_context(nc.Block())
        per_eng = {e: [] for e in self.ENGS}
        for (eng, fn, waits, inc) in self.ops:
            per_eng[eng].append((fn, waits, inc))

        def make(eng_name):
            def body(engobj):
                for (fn, waits, inc) in per_eng[eng_name]:
                    for s, v in waits.items():
                        engobj.wait_ge(sems[s], v)
                    ins = fn(engobj)
                    ins.then_inc(sems[inc[0]], inc[1])
                if eng_name == "sync":
                    for (s, v) in final:
                        engobj.wait_ge(sems[s], v)
            return body

        for e in self.ENGS:
            if per_eng[e] or e == "sync":
                getattr(block, e)(make(e))


class Rot:
    def __init__(self, items):
        self.items = items
        self.i = 0

    def next(self):
        it = self.items[self.i % len(self.items)]
        self.i += 1
        return it


def build_program(dbg=False, upto=99):
    nc = bass.Bass("TRN2", target_bir_lowering=False)

    def din(name, shape, dt=F32):
        return nc.dram_tensor(name, list(shape), dt, kind="ExternalInput").ap()

    def dout(name, shape, dt=F32):
        return nc.dram_tensor(name, list(shape), dt, kind="ExternalOutput").ap()

    def dscr(name, shape, dt=F32):
        return nc.dram_tensor(name, list(shape), dt, kind="ExternalOutput" if dbg else "Internal").ap()

    xm = din("xm", [NTOK, DM])
    xpre = din("xpre", [NPRE, DM])
    ctok = din("ctok", [256, DM])
    flag_d = din("flag", [128, 1])
    st_h = din("st_h", [NSEQ, 1024])
    st_conv = din("st_conv", [NSEQ * 3, 1024])
    st_sc = din("st_sc", [NSEQ * 2, 1024])
    w_ada = din("w_ada", [DM, 6 * DM])
    b_ada = din("b_ada", [1, 6 * DM])
    gvecs = din("gvecs", [3, DM])
    w_in = din("w_in", [DM, 5120])
    chvec = din("chvec", [13, 1024])
    lru_wa = din("lru_wa", [16, 64, 64])
    lru_wx = din("lru_wx", [16, 64, 64])
    w_out = din("w_out", [DM, DM])
    wq = din("wq", [DM, DM])
    keys = din("keys", [256, 128])
    puT = din("puT", [DM, NEXP])
    pv = din("pv", [NEXP, DM])

    y_d = dout("y", [NTOK, DM])
    hlp_d = dout("hl_p", [8, 128])
    cvp_d = dout("cv_p", [3, 1024])
    scp_d = dout("sc_p", [2, 1024])
    hls_d = dout("hl_s", [NSEQ, 1024])
    cvs_d = dout("cv_s", [NSEQ * 3, 1024])
    scs_d = dout("sc_s", [NSEQ * 2, 1024])

    pre_d = dscr("pre_d", [16, 128, NTOK])
    x1_d = dscr("x1_d", [NTOK, DM])
    mods_d = dscr("mods_d", [4, 2, 128, DM])
    if dbg:
        dbg_hT = dout("dbg_hT", [128, 16, NTOK], BF16)
        dbg_mix = dout("dbg_mix", [128, 16, NTOK], BF16)
        dbg_h2T = dout("dbg_h2T", [128, 16, NTOK], BF16)
        dbg_qT = dout("dbg_qT", [128, 16, NTOK], BF16)
        dbg_th = dout("dbg_th", [128, 2, 72])
        dbg_PT = dout("dbg_PT", [128, 16, NTOK])

    _uid = [0]

    def uq(name):
        _uid[0] += 1
        return "%s_%d" % (name, _uid[0])

    S = Sched(nc)
    V = lambda fn, r=(), w=(): S.op("vector", fn, r, w)
    A = lambda fn, r=(), w=(): S.op("scalar", fn, r, w)
    G = lambda fn, r=(), w=(): S.op("gpsimd", fn, r, w)
    T = lambda fn, r=(), w=(): S.op("tensor", fn, r, w)

    def DMA(out, in_, key, r=(), w=(), eng="sync"):
        S.dma(eng, lambda e: e.dma_start(out=out, in_=in_), key, r, w)

    gst = ExitStack()
    with gst:
        def gsb(name, shape, dt):
            return gst.enter_context(nc.sbuf_tensor(uq(name), list(shape), dt))

        pf = [gst.enter_context(nc.psum_tensor("pf%d" % i, [128, 512], F32)) for i in range(6)]
        pb = [gst.enter_context(nc.psum_tensor("pb%d" % i, [128, 1024], BF16)) for i in range(2)]
        pm = Rot([(pf[i], "pf%d" % i) for i in range(4)])
        pa = Rot([(pf[4], "pf4"), (pf[5], "pf5")])
        pt = Rot([(pb[0], "pb0"), (pb[1], "pb1")])

        ident_f = gsb("ident_f", [128, 128], F32)
        ident_b = gsb("ident_b", [128, 128], BF16)
        ones_f = gsb("ones_f", [128, 128], F32)
        chv = gsb("chv", [128, 8, 16], F32)
        flag = gsb("flag_sb", [128, 1], F32)
        h2T = gsb("h2T", [128, 16, NTOK], BF16)
        KT = gsb("KT", [128, 2, 128], BF16)
        stats = gsb("stats", [128, 64, 4], F32)
        stat_i = [0]

        G(lambda e: e.memset(ident_f[:], 0.0), w=["ident_f"])
        G(lambda e: e.affine_select(out=ident_f[:], in_=ident_f[:], pattern=[[-1, 128]],
                                    compare_op=ALU.not_equal, fill=1.0, base=0, channel_multiplier=1),
          r=["ident_f"], w=["ident_f"])
        V(lambda e: e.tensor_copy(out=ident_b[:], in_=ident_f[:]), r=["ident_f"], w=["ident_b"])
        V(lambda e: e.memset(ones_f[:], 1.0), w=["ones_f"])
        DMA(flag[:], flag_d, "flag", w=["flag"])

        tst = ExitStack()
        with tst:
            def tsb(name, shape, dt):
                return tst.enter_context(nc.sbuf_tensor(uq(name), list(shape), dt))

            BDa = tsb("BDa", [128, 8, 128], BF16)
            BDx = tsb("BDx", [128, 8, 128], BF16)
            stT = tsb("stT", [128, 8, 96], F32)
            cT = tsb("cT", [128, 2, 16, 128], BF16)
            hT = tsb("hT", [128, 16, NTOK], BF16)
            hTp = tsb("hTp", [128, 16, NPRE], BF16)
            sst = ExitStack()
            sst.__enter__()
            tsb_outer = tsb
            tsb = lambda name, shape, dt: sst.enter_context(nc.sbuf_tensor(uq(name), list(shape), dt))
            chv_tm = tsb("chv_tm", [13, 1024], F32)
            DMA(chv_tm[:], chvec, "chv_tm", w=["chv_tm"])
            for c in range(8):
                p, pn = pa.next()
                T(lambda e, p=p, c=c: e.transpose(out=p[:, 0:13], in_=chv_tm[0:13, c * 128:(c + 1) * 128],
                                                  identity=ident_f[0:13, 0:13]),
                  r=["chv_tm", "ident_f"], w=[pn])
                V(lambda e, p=p, c=c: e.tensor_copy(out=chv[:, c, 0:13], in_=p[:, 0:13]), r=[pn], w=["chv"])
            A(lambda e: e.activation(out=chv[:, :, 13], in_=chv[:, :, 7], func=AF.Exp, scale=-1.0), r=["chv"], w=["chv"])
            A(lambda e: e.activation(out=chv[:, :, 13], in_=chv[:, :, 13], func=AF.Ln, bias=1.0), r=["chv"], w=["chv"])
            V(lambda e: e.tensor_scalar(out=chv[:, :, 14], in0=chv[:, :, 13], scalar1=-16.0, scalar2=None, op0=ALU.mult),
              r=["chv"], w=["chv"])
            V(lambda e: e.tensor_scalar(out=chv[:, :, 13], in0=chv[:, :, 13], scalar1=-8.0, scalar2=None, op0=ALU.mult),
              r=["chv"], w=["chv"])
            keys_tm = tsb("keys_tm", [128, 2, 128], F32)
            for p_ in range(2):
                DMA(keys_tm[:, p_, :], keys[p_ * 128:(p_ + 1) * 128, :], "keys_tm", w=["keys_tm"])
            for p_ in range(2):
                p, pn = pa.next()
                T(lambda e, p=p, p_=p_: e.transpose(out=p[:, 0:128], in_=keys_tm[:, p_, :], identity=ident_f[:]),
                  r=["keys_tm", "ident_f"], w=[pn])
                V(lambda e, p=p, p_=p_: e.tensor_copy(out=KT[:, p_, :], in_=p[:, 0:128]), r=[pn], w=["KT"])
            st_tm = tsb("st_tm", [96, 1024], F32)
            DMA(st_tm[0:48, :], st_conv, "st_tm", w=["st_tm"])
            DMA(st_tm[48:80, :], st_sc, "st_tm", w=["st_tm"])
            DMA(st_tm[80:96, :], st_h, "st_tm", w=["st_tm"])
            for c in range(8):
                p, pn = pa.next()
                T(lambda e, p=p, c=c: e.transpose(out=p[:, 0:96], in_=st_tm[0:96, c * 128:(c + 1) * 128],
                                                  identity=ident_f[0:96, 0:96]),
                  r=["st_tm", "ident_f"], w=[pn])
                V(lambda e, p=p, c=c: e.tensor_copy(out=stT[:, c, :], in_=p[:, 0:96]), r=[pn], w=["stT"])

            S.barrier()
            sst.__exit__(None, None, None)
            nst = ExitStack()
            nst.__enter__()
            tsb = lambda name, shape, dt: nst.enter_context(nc.sbuf_tensor(uq(name), list(shape), dt))
            xt_pool = Rot([(tsb("xt0", [128, DM], F32), "xt0"), (tsb("xt1", [128, DM], F32), "xt1")])
            hb_pool = Rot([(tsb("hb0", [128, DM], BF16), "hb0"), (tsb("hb1", [128, DM], BF16), "hb1")])

            def transpose16(src_bf, src_res, dst_fn, dst_res):
                for b in range(2):
                    p, pn = pt.next()
                    for j in range(8):
                        dc = b * 8 + j
                        T(lambda e, p=p, j=j, dc=dc: e.transpose(out=p[:, j * 128:(j + 1) * 128],
                                                                 in_=src_bf[:, dc * 128:(dc + 1) * 128],
                                                                 identity=ident_b[:]),
                          r=[src_res, "ident_b"], w=[pn])
                    if b == 0:
                        A(lambda e, p=p, b=b: e.activation(out=dst_fn(b), in_=p[:].rearrange("p (j t) -> p j t", t=128),
                                                           func=AF.Copy), r=[pn], w=[dst_res])
                    else:
                        V(lambda e, p=p, b=b: e.tensor_copy(out=dst_fn(b), in_=p[:].rearrange("p (j t) -> p j t", t=128)),
                          r=[pn], w=[dst_res])

            for g in range(2):
                xt, xn = xt_pool.next()
                hb, hn = hb_pool.next()
                DMA(xt[:], ctok[g * 128:(g + 1) * 128, :], xn, w=[xn])
                A(lambda e, xt=xt, hb=hb: e.activation(out=hb[:], in_=xt[:], func=AF.Silu), r=[xn], w=[hn])
                transpose16(hb, hn, lambda b, g=g: cT[:, g, b * 8:(b + 1) * 8, :], "cT")

            modA = [tsb("modA%d" % g, [128, DM], F32) for g in range(2)]
            modB = [tsb("modB%d" % g, [128, DM], F32) for g in range(2)]
            gb = tsb("gb", [128, DM], F32)
            bada_sb = tsb("bada_sb", [1, 1024], F32)
            wada_pool = Rot([(tsb("wada%d" % i, [128, 1024], BF16), "wada%d" % i) for i in range(5)])

            def mod_piece(piece, dst, dst_names, gs_row=None):
                col0 = piece * DM
                gb_ = gb
                bada_ = bada_sb
                if gs_row is not None:
                    DMA(gb[:], gvecs[gs_row:gs_row + 1, :].to_broadcast([128, DM]), "gb", w=["gb"])
                for half in range(2):
                    banks = [pm.next() for _ in range(4)]
                    DMA(bada_sb[:], b_ada[:, col0 + half * 1024:col0 + (half + 1) * 1024], "bada", w=["bada"])
                    for kc in range(16):
                        wb, wn = wada_pool.next()
                        DMA(wb[:], w_ada[kc * 128:(kc + 1) * 128, col0 + half * 1024: col0 + (half + 1) * 1024], wn,
                            w=[wn], eng="gpsimd")
                        for g in range(2):
                            for nb in range(2):
                                p, pn = banks[g * 2 + nb]
                                T(lambda e, p=p, g=g, nb=nb, kc=kc, wb=wb: e.matmul(
                                    p[:], lhsT=cT[:, g, kc, :], rhs=wb[:, nb * 512:(nb + 1) * 512],
                                    start=(kc == 0), stop=False), r=["cT", wn], w=[pn])
                    for g in range(2):
                        for nb in range(2):
                            p, pn = banks[g * 2 + nb]
                            cc = half * 1024 + nb * 512
                            T(lambda e, p=p, nb=nb: e.matmul(p[:], lhsT=ones_f[0:1, :], rhs=bada_[0:1, nb * 512:(nb + 1) * 512],
                                                             start=False, stop=True), r=["ones_f", "bada"], w=[pn])
                            if gs_row is None:
                                A(lambda e, p=p, g=g, cc=cc: e.activation(out=dst[g][:, cc:cc + 512], in_=p[:], func=AF.Copy),
                                  r=[pn], w=[dst_names[g]])
                            else:
                                V(lambda e, p=p, g=g, cc=cc: e.scalar_tensor_tensor(
                                    out=dst[g][:, cc:cc + 512], in0=p[:], scalar=1.0, in1=gb_[:, cc:cc + 512],
                                    op0=ALU.add, op1=ALU.mult), r=[pn, "gb"], w=[dst_names[g]])

            stg_pool = Rot([(tsb("stg%d" % i, [128, 512], F32), "stg%d" % i) for i in range(2)])

            def mod_stream(pieces, LOOK=4):
                items = [(idx, pc, half, kc) for (idx, pc) in pieces for half in (0, 1) for kc in range(16)]
                loaded = {}
                hold = {}

                def issue_load(j):
                    idx, pc, half, kc = items[j]
                    wb, wn = wada_pool.next()
                    DMA(wb[:], w_ada[kc * 128:(kc + 1) * 128, pc * DM + half * 1024: pc * DM + (half + 1) * 1024], wn, w=[wn], eng="gpsimd")
                    loaded[j] = (wb, wn)

                def step(j):
                    idx, pc, half, kc = items[j]
                    if j == 0:
                        for jj in range(min(LOOK, len(items))):
                            issue_load(jj)
                    if j + LOOK < len(items):
                        issue_load(j + LOOK)
                    if kc == 0:
                        hold["banks"] = [pm.next() for _ in range(4)]
                        DMA(bada_sb[:], b_ada[:, pc * DM + half * 1024:pc * DM + (half + 1) * 1024], "bada", w=["bada"])
                    banks = hold["banks"]
                    wb, wn = loaded.pop(j)
                    bada_ = bada_sb
                    for g in range(2):
                        for nb in range(2):
                            p, pn = banks[g * 2 + nb]
                            T(lambda e, p=p, g=g, nb=nb, kc=kc, wb=wb: e.matmul(
                                p[:], lhsT=cT[:, g, kc, :], rhs=wb[:, nb * 512:(nb + 1) * 512], start=(kc == 0), stop=False),
                              r=["cT", wn], w=[pn])
                    if kc == 15:
                        for g in range(2):
                            for nb in range(2):
                                p, pn = banks[g * 2 + nb]
                                cc = half * 1024 + nb * 512
                                T(lambda e, p=p, nb=nb: e.matmul(p[:], lhsT=ones_f[0:1, :], rhs=bada_[0:1, nb * 512:(nb + 1) * 512],
                                                                 start=False, stop=True), r=["ones_f", "bada"], w=[pn])
                                sg, sgn = stg_pool.next()
                                A(lambda e, p=p, sg=sg: e.activation(out=sg[:], in_=p[:], func=AF.Copy), r=[pn], w=[sgn])
                                DMA(mods_d[idx, g, :, cc:cc + 512], sg[:], sgn, r=[sgn], w=["mods_d%d" % idx])

                for j in range(len(items)):
                    yield (lambda j=j: step(j))

            def norm_to_T(src_ap, grp, dstT, dst_res, col0, GS, GSn, SH, SHn, src_reads=()):
                xt, xn = xt_pool.next()
                hb, hn = hb_pool.next()
                si = stat_i[0] % 64
                stat_i[0] += 1
                sres = "stat%d" % si
                DMA(xt[:], src_ap, xn, r=list(src_reads), w=[xn])
                A(lambda e: e.activation(out=hb[:], in_=xt[:], func=AF.Square, accum_out=stats[:, si, 0:1]),
                  r=[xn], w=[hn, sres])
                A(lambda e: e.activation(out=stats[:, si, 1:2], in_=stats[:, si, 0:1], func=AF.Sqrt, scale=1.0 / DM, bias=EPS),
                  r=[sres], w=[sres])
                V(lambda e: e.reciprocal(out=stats[:, si, 2:3], in_=stats[:, si, 1:2]), r=[sres], w=[sres])
                V(lambda e: e.scalar_tensor_tensor(out=xt[:], in0=xt[:], scalar=stats[:, si, 2:3], in1=GS[grp][:],
                                                   op0=ALU.mult, op1=ALU.mult), r=[xn, sres, GSn[grp]], w=[xn])
                HS = 1280
                V(lambda e: e.tensor_tensor(out=hb[:, 0:HS], in0=xt[:, 0:HS], in1=SH[grp][:, 0:HS], op=ALU.add), r=[xn, SHn[grp]], w=[hn + "a"])
                G(lambda e: e.tensor_tensor(out=hb[:, HS:DM], in0=xt[:, HS:DM], in1=SH[grp][:, HS:DM], op=ALU.add), r=[xn, SHn[grp]], w=[hn + "b"])
                S.op("vector", lambda e: e.memset(stats[:, si, 3:4], 0.0), [hn + "a", hn + "b"], [hn])
                return lambda: transpose16(hb, hn, lambda b: dstT[:, b * 8:(b + 1) * 8, col0:col0 + 128], dst_res)

            def run_pipelined(parts, extra=None, per=0):
                pend = None
                for a in parts:
                    b = a()
                    if pend is not None:
                        pend()
                    pend = b
                    if extra is not None:
                        for _ in range(per):
                            st = next(extra, None)
                            if st is not None:
                                st()
                if pend is not None:
                    pend()
                if extra is not None:
                    for st in extra:
                        st()

            mAn = ["modA0", "modA1"]
            mBn = ["modB0", "modB1"]
            mod_piece(1, modA, mAn, gs_row=0)
            mod_piece(0, modB, mBn)
            parts = []
            for i in range(NPRE // 128):
                parts.append(lambda i=i: norm_to_T(xpre[i * 128:(i + 1) * 128, :], 0, hTp, "hTp", i * 128, modA, mAn, modB, mBn))
            for i in range(NTILE):
                parts.append(lambda i=i: norm_to_T(xm[i * 128:(i + 1) * 128, :], 0 if i < 8 else 1, hT, "hT", i * 128, modA, mAn, modB, mBn))
            run_pipelined(parts, extra=mod_stream([(0, 2), (1, 4), (2, 3), (3, 5)]), per=6)
            if dbg:
                DMA(dbg_hT, hT[:], "dbg_hT", r=["hT"])
            S.barrier()
            nst.__exit__(None, None, None)

            V(lambda e: e.memset(BDa[:], 0.0), w=["BDa"])
            V(lambda e: e.memset(BDx[:], 0.0), w=["BDx"])
            for c in range(8):
                for hh in range(2):
                    DMA(BDa[hh * 64:(hh + 1) * 64, c, hh * 64:(hh + 1) * 64], lru_wa[2 * c + hh], "BDa",
                        r=[], w=["BDa"], eng="gpsimd")
                    DMA(BDx[hh * 64:(hh + 1) * 64, c, hh * 64:(hh + 1) * 64], lru_wx[2 * c + hh], "BDx",
                        r=[], w=["BDx"], eng="gpsimd")
            lsc = ExitStack()
            lsc.__enter__()
            tsb = lambda name, shape, dt: lsc.enter_context(nc.sbuf_tensor(uq(name), list(shape), dt))
            XB = tsb("XB", [128, 1027], F32)
            XSB_ = tsb("XSs", [128, NSEQ * 11], F32)
            XC = tsb("XC", [128, 1024], F32)
            XCB = tsb("XCB", [128, 1024], BF16)
            RB = tsb("RB", [128, 1024], F32)
            IB = tsb("IB", [128, 1024], F32)
            AB = tsb("AB", [128, 1024], F32)
            A2 = tsb("A2", [128, 1024], F32)
            HB = tsb("HB", [128, 1024], F32)
            GY = tsb("GY", [128, NTOK], F32)
            PRE = tsb("PRE", [128, NTOK], F32)
            SQ = tsb("SQ", [128, NTOK], F32)
            ssq = [tsb("ssq%d" % g, [128, NTOK], F32) for g in range(2)]
            hpre = tsb("hpre", [128, 8], F32)
            xtail = tsb("xtail", [128, 8, 3], F32)
            cxtail = tsb("cxtail", [128, 8, 2], F32)
            hlM = tsb("hlM", [128, 8], F32)
            hlS = tsb("hlS", [128, 8, NSEQ], F32)
            cvM = tsb("cvM", [128, 8, 3], F32)
            cvS = tsb("cvS", [128, 8, NSEQ * 3], F32)
            scM = tsb("scM", [128, 8, 2], F32)
            scS = tsb("scS", [128, 8, NSEQ * 2], F32)
            win_pool = Rot([(tsb("win%d" % i, [128, 16, 128], BF16), "win%d" % i) for i in range(4)])

            def load_w_chunk(wsrc, j):
                wb, wn = win_pool.next()
                DMA(wb[:], wsrc[:, j * 128:(j + 1) * 128].rearrange("(kc p) n -> p kc n", p=128), wn, w=[wn], eng="gpsimd")
                return wb, wn

            def proj(wb, wn, srcT, src_res, c0, n):
                p, pn = pm.next()
                for kc in range(16):
                    T(lambda e, p=p, kc=kc: e.matmul(p[:, 0:n], lhsT=wb[:, kc, :], rhs=srcT[:, kc, c0:c0 + n],
                                                     start=(kc == 0), stop=(kc == 15)), r=[wn, src_res], w=[pn])
                return p, pn

            def lru_core(c, Xv, Xres, nseq, L, h0_ap, h0_res, sample):
                N = nseq * L
                v3 = lambda buf: buf[:, 0:N].rearrange("p (s l) -> p s l", l=L)
                V(lambda e: e.tensor_scalar(out=v3(XC), in0=Xv[:, :, 0:L], scalar1=chv[:, c, 0:1], scalar2=chv[:, c, 4:5],
                                            op0=ALU.mult, op1=ALU.add), r=[Xres, "chv"], w=["XC0"])
                for k in range(1, 4):
                    V(lambda e, k=k: e.scalar_tensor_tensor(out=v3(XC), in0=Xv[:, :, k:k + L], scalar=chv[:, c, k:k + 1],
                                                            in1=v3(XC), op0=ALU.mult, op1=ALU.add),
                      r=[Xres, "chv", "XC0"], w=["XC0"])
                A(lambda e: e.activation(out=XCB[:, 0:N], in_=XC[:, 0:N], func=AF.Copy), r=["XC0"], w=["XCB0"])
                for n0 in range(0, N, 512):
                    n = min(512, N - n0)
                    p, pn = pa.next()
                    T(lambda e, p=p, n0=n0, n=n: e.matmul(p[:, 0:n], lhsT=BDa[:, c, :], rhs=XCB[:, n0:n0 + n], start=True, stop=True),
                      r=["BDa", "XCB0"], w=[pn])
                    A(lambda e, p=p, n0=n0, n=n: e.activation(out=RB[:, n0:n0 + n], in_=p[:, 0:n], func=AF.Sigmoid,
                                                              bias=chv[:, c, 5:6]), r=[pn, "chv"], w=["RB0"])
                    p, pn = pa.next()
                    T(lambda e, p=p, n0=n0, n=n: e.matmul(p[:, 0:n], lhsT=BDx[:, c, :], rhs=XCB[:, n0:n0 + n], start=True, stop=True),
                      r=["BDx", "XCB0"], w=[pn])
                    A(lambda e, p=p, n0=n0, n=n: e.activation(out=IB[:, n0:n0 + n], in_=p[:, 0:n], func=AF.Sigmoid,
                                                              bias=chv[:, c, 6:7]), r=[pn, "chv"], w=["IB0"])
                A(lambda e: e.activation(out=AB[:, 0:N], in_=RB[:, 0:N], func=AF.Exp, scale=chv[:, c, 13:14]), r=["RB0", "chv"], w=["AB0"])
                A(lambda e: e.activation(out=A2[:, 0:N], in_=RB[:, 0:N], func=AF.Exp, scale=chv[:, c, 14:15]), r=["RB0", "chv"], w=["A20"])
                A(lambda e: e.activation(out=A2[:, 0:N], in_=A2[:, 0:N], func=AF.Sqrt, scale=-1.0, bias=1.0), r=["A20"], w=["A20"])
                V(lambda e: e.tensor_tensor(out=IB[:, 0:N], in0=IB[:, 0:N], in1=XC[:, 0:N], op=ALU.mult), r=["IB0", "XC0"], w=["IB0"])
                V(lambda e: e.tensor_tensor(out=IB[:, 0:N], in0=IB[:, 0:N], in1=A2[:, 0:N], op=ALU.mult), r=["IB0", "A20"], w=["IB0"])
                if sample:
                    V(lambda e: e.tensor_tensor(out=v3(A2)[:, :, 0], in0=v3(AB)[:, :, 0], in1=h0_ap, op=ALU.mult),
                      r=["AB0", h0_res], w=["A20"])
                    V(lambda e: e.tensor_tensor(out=v3(IB)[:, :, 0], in0=v3(IB)[:, :, 0], in1=v3(A2)[:, :, 0], op=ALU.add),
                      r=["IB0", "A20"], w=["IB0"])
                    V(lambda e: e.memset(v3(AB)[:, :, 0], 0.0), r=["A20"], w=["AB0"])
                    V(lambda e: e.tensor_tensor_scan(out=HB[:, 0:N], data0=AB[:, 0:N], data1=IB[:, 0:N], initial=0.0,
                                                     op0=ALU.mult, op1=ALU.add), r=["AB0", "IB0"], w=["HB0"])
                else:
                    init = 0.0 if h0_ap is None else h0_ap
                    V(lambda e: e.tensor_tensor_scan(out=HB[:, 0:N], data0=AB[:, 0:N], data1=IB[:, 0:N], initial=init,
                                                     op0=ALU.mult, op1=ALU.add),
                      r=["AB0", "IB0"] + ([h0_res] if h0_ap is not None else []), w=["HB0"])

            SXC = tsb("SXC", [128, 128], F32)
            SXCB = tsb("SXCB", [128, 128], BF16)
            SRB = tsb("SRB", [128, 128], F32)
            SIB = tsb("SIB", [128, 128], F32)
            SAB = tsb("SAB", [128, 128], F32)
            SA2 = tsb("SA2", [128, 128], F32)
            SHB = tsb("SHB", [128, 128], F32)

            def lru_split(c, h0_ap, h0_res, XBt, xbn, smp=None):
                H = 512
                halves = (0, 1)
                rn = lambda base, hf: "%s%d" % (base, hf)
                sl = lambda buf, hf: buf[:, hf * H:(hf + 1) * H]
                s3 = lambda buf: buf[:, 0:128].rearrange("p (s l) -> p s l", l=LS)
                if smp is not None:
                    sXv, sxres, sh0, sh0res = smp
                for hf in halves:
                    xr = [xbn + "0"] if hf == 0 else [xbn + "0", xbn + "1"]
                    V(lambda e, hf=hf: e.tensor_scalar(out=sl(XC, hf), in0=XBt[:, hf * H:hf * H + H], scalar1=chv[:, c, 0:1],
                                                       scalar2=chv[:, c, 4:5], op0=ALU.mult, op1=ALU.add), r=xr + ["chv"], w=[rn("XC", hf)])
                    for k in range(1, 4):
                        V(lambda e, hf=hf, k=k: e.scalar_tensor_tensor(out=sl(XC, hf), in0=XBt[:, hf * H + k:hf * H + k + H],
                                                                       scalar=chv[:, c, k:k + 1], in1=sl(XC, hf), op0=ALU.mult, op1=ALU.add),
                          r=xr + ["chv", rn("XC", hf)], w=[rn("XC", hf)])
                if smp is not None:
                    V(lambda e: e.tensor_scalar(out=s3(SXC), in0=sXv[:, :, 0:LS], scalar1=chv[:, c, 0:1], scalar2=chv[:, c, 4:5],
                                                op0=ALU.mult, op1=ALU.add), r=[sxres, "chv"], w=["SXC"])
                    for k in range(1, 4):
                        V(lambda e, k=k: e.scalar_tensor_tensor(out=s3(SXC), in0=sXv[:, :, k:k + LS], scalar=chv[:, c, k:k + 1],
                                                                in1=s3(SXC), op0=ALU.mult, op1=ALU.add), r=[sxres, "chv", "SXC"], w=["SXC"])
                for hf in halves:
                    A(lambda e, hf=hf: e.activation(out=sl(XCB, hf), in_=sl(XC, hf), func=AF.Copy), r=[rn("XC", hf)], w=[rn("XCB", hf)])
                if smp is not None:
                    A(lambda e: e.activation(out=SXCB[:], in_=SXC[:], func=AF.Copy), r=["SXC"], w=["SXCB"])
                jobs = [(sl(XCB, hf), rn("XCB", hf), sl(RB, hf), rn("RB", hf), sl(IB, hf), rn("IB", hf), H) for hf in halves]
                if smp is not None:
                    jobs.append((SXCB[:], "SXCB", SRB[:], "SRB", SIB[:], "SIB", 128))
                for (xin, xinr, rout, routr, iout, ioutr, n) in jobs:
                    p, pn = pa.next()
                    T(lambda e, p=p, xin=xin, n=n: e.matmul(p[:, 0:n], lhsT=BDa[:, c, :], rhs=xin, start=True, stop=True),
                      r=["BDa", xinr], w=[pn])
                    A(lambda e, p=p, rout=rout, n=n: e.activation(out=rout, in_=p[:, 0:n], func=AF.Sigmoid, bias=chv[:, c, 5:6]),
                      r=[pn, "chv"], w=[routr])
                    p, pn = pa.next()
                    T(lambda e, p=p, xin=xin, n=n: e.matmul(p[:, 0:n], lhsT=BDx[:, c, :], rhs=xin, start=True, stop=True),
                      r=["BDx", xinr], w=[pn])
                    A(lambda e, p=p, iout=iout, n=n: e.activation(out=iout, in_=p[:, 0:n], func=AF.Sigmoid, bias=chv[:, c, 6:7]),
                      r=[pn, "chv"], w=[ioutr])
                ej = [(sl(RB, hf), rn("RB", hf), sl(AB, hf), rn("AB", hf), sl(A2, hf), rn("A2", hf)) for hf in halves]
                if smp is not None:
                    ej.append((SRB[:], "SRB", SAB[:], "SAB", SA2[:], "SA2"))
                for (rin, rinr, aout, aoutr, a2out, a2r) in ej:
                    A(lambda e, rin=rin, aout=aout: e.activation(out=aout, in_=rin, func=AF.Exp, scale=chv[:, c, 13:14]),
                      r=[rinr, "chv"], w=[aoutr])
                    A(lambda e, rin=rin, a2out=a2out: e.activation(out=a2out, in_=rin, func=AF.Exp, scale=chv[:, c, 14:15]),
                      r=[rinr, "chv"], w=[a2r])
                for (rin, rinr, aout, aoutr, a2out, a2r) in ej:
                    A(lambda e, a2out=a2out: e.activation(out=a2out, in_=a2out, func=AF.Sqrt, scale=-1.0, bias=1.0), r=[a2r], w=[a2r])
                uj = [(sl(IB, hf), rn("IB", hf), sl(XC, hf), rn("XC", hf), sl(A2, hf), rn("A2", hf)) for hf in halves]
                if smp is not None:
                    uj.append((SIB[:], "SIB", SXC[:], "SXC", SA2[:], "SA2"))
                for (ib, ibr, xc_, xcr, a2_, a2r) in uj:
                    V(lambda e, ib=ib, xc_=xc_: e.tensor_tensor(out=ib, in0=ib, in1=xc_, op=ALU.mult), r=[ibr, xcr], w=[ibr])
                    V(lambda e, ib=ib, a2_=a2_: e.tensor_tensor(out=ib, in0=ib, in1=a2_, op=ALU.mult), r=[ibr, a2r], w=[ibr])
                for hf in halves:
                    if hf == 0:
                        init = 0.0 if h0_ap is None else h0_ap
                        rr = [h0_res] if h0_ap is not None else []
                    else:
                        init = HB[:, H - 1:H]
                        rr = ["HB0"]
                    V(lambda e, hf=hf, init=init: e.tensor_tensor_scan(out=sl(HB, hf), data0=sl(AB, hf), data1=sl(IB, hf), initial=init,
                                                                       op0=ALU.mult, op1=ALU.add),
                      r=[rn("AB", hf), rn("IB", hf)] + rr, w=[rn("HB", hf)])
                if smp is not None:
                    V(lambda e: e.tensor_tensor(out=s3(SA2)[:, :, 0], in0=s3(SAB)[:, :, 0], in1=sh0, op=ALU.mult),
                      r=["SAB", "SA2", sh0res], w=["SA2"])
                    V(lambda e: e.tensor_tensor(out=s3(SIB)[:, :, 0], in0=s3(SIB)[:, :, 0], in1=s3(SA2)[:, :, 0], op=ALU.add),
                      r=["SIB", "SA2"], w=["SIB"])
                    V(lambda e: e.memset(s3(SAB)[:, :, 0], 0.0), r=["SA2"], w=["SAB"])
                    V(lambda e: e.tensor_tensor_scan(out=SHB[:], data0=SAB[:], data1=SIB[:], initial=0.0, op0=ALU.mult, op1=ALU.add),
                      r=["SAB", "SIB"], w=["SHB"])

            XBv = XB[:, :].rearrange("p (s l) -> p s l", s=1)
            XSv = XSB_[:, :].rearrange("p (s l) -> p s l", l=11)
            XB2 = tsb("XB2", [128, 1027], F32)
            XS2 = tsb("XS2", [128, NSEQ * 11], F32)
            GY2 = tsb("GY2", [128, NTOK], F32)
            bsets = [dict(XB=XB, xbn="XB", XS=XSB_, xsn="XSs", GY=GY, gyn="GY"),
                     dict(XB=XB2, xbn="XBb", XS=XS2, xsn="XSb", GY=GY2, gyn="GYb")]

            if True:
                def P_pre(c, st):
                    XBt, xbn = st["XB"], st["xbn"]
                    wb, wn = load_w_chunk(w_in, c)
                    V(lambda e: e.memset(XBt[:, 0:3], 0.0), w=[xbn + "0"])
                    for nb in range(2):
                        p, pn = proj(wb, wn, hTp, "hTp", nb * 512, 512)
                        A(lambda e, p=p, nb=nb: e.activation(out=XBt[:, 3 + nb * 512:3 + (nb + 1) * 512], in_=p[:], func=AF.Copy),
                          r=[pn], w=[xbn + "%d" % nb])

                def E_pre(c, st):
                    XBt, xbn = st["XB"], st["xbn"]
                    lru_split(c, None, None, XBt, xbn)
                    V(lambda e: e.tensor_scalar(out=hpre[:, c:c + 1], in0=HB[:, 1023:1024], scalar1=flag[:, 0:1], scalar2=None,
                                                op0=ALU.mult), r=["HB1", "flag"], w=["hpre"])
                    V(lambda e: e.tensor_scalar(out=xtail[:, c, :], in0=XBt[:, 1024:1027], scalar1=flag[:, 0:1], scalar2=None,
                                                op0=ALU.mult), r=[xbn + "1", "flag"], w=["xtail"])

                P_pre(0, bsets[0])
                for c in range(8):
                    if c + 1 < 8:
                        P_pre(c + 1, bsets[(c + 1) % 2])
                    E_pre(c, bsets[c % 2])
                for c in range(8):
                    wbx, wnx = load_w_chunk(w_in, 32 + c)
                    wbc, wnc = load_w_chunk(w_in, 24 + c)
                    px, pxn = proj(wbx, wnx, hTp, "hTp", NPRE - 8, 8)
                    A(lambda e, px=px: e.activation(out=SQ[:, 0:8], in_=px[:, 0:8], func=AF.Copy), r=[pxn], w=["SQ"])
                    pc, pcn = proj(wbc, wnc, hTp, "hTp", NPRE - 8, 8)
                    V(lambda e, pc=pc: e.tensor_tensor(out=SQ[:, 0:8], in0=pc[:, 0:8], in1=SQ[:, 0:8], op=ALU.mult),
                      r=[pcn, "SQ"], w=["SQ"])
                    V(lambda e, c=c: e.tensor_scalar(out=cxtail[:, c, :], in0=SQ[:, 6:8], scalar1=flag[:, 0:1], scalar2=None,
                                                     op0=ALU.mult), r=["SQ", "flag"], w=["cxtail"])
                S.barrier()


            def finish_chunk(cidx, g, first):
                DMA(pre_d[cidx], PRE[:], "pre_d%d" % cidx, r=["PRE"], w=["pre_d%d" % cidx])
                A(lambda e: e.activation(out=SQ[:], in_=PRE[:], func=AF.Square), r=["PRE"], w=["SQ"])
                for (n0, n) in ((0, 512), (512, 512), (1024, 128)):
                    p, pn = pa.next()
                    T(lambda e, p=p, n0=n0, n=n: e.matmul(p[:, 0:n], lhsT=ones_f[:], rhs=SQ[:, n0:n0 + n], start=True, stop=True),
                      r=["ones_f", "SQ"], w=[pn])
                    if first:
                        V(lambda e, p=p, n0=n0, n=n: e.tensor_copy(out=ssq[g][:, n0:n0 + n], in_=p[:, 0:n]), r=[pn], w=["ssq%d" % g])
                    else:
                        V(lambda e, p=p, n0=n0, n=n: e.tensor_tensor(out=ssq[g][:, n0:n0 + n], in0=p[:, 0:n], in1=ssq[g][:, n0:n0 + n],
                                                                     op=ALU.add), r=[pn, "ssq%d" % g], w=["ssq%d" % g])

            blocks = ((0, 512), (512, 512), (1024, 128))

            def P_main(c, st):
                XBt, xbn, XSt, xsn, GYt, gyn = st["XB"], st["xbn"], st["XS"], st["xsn"], st["GY"], st["gyn"]
                XSv_ = XSt[:, :].rearrange("p (s l) -> p s l", l=11)
                wb, wn = load_w_chunk(w_in, c)
                V(lambda e: e.tensor_copy(out=XBt[:, 0:3], in_=xtail[:, c, :]), r=["xtail"], w=[xbn + "0"])
                V(lambda e: e.tensor_copy(out=XSv_[:, :, 0:3], in_=stT[:, c, 0:48].rearrange("p (s k) -> p s k", k=3)),
                  r=["stT"], w=[xsn])
                for (n0, n) in blocks:
                    p, pn = proj(wb, wn, hT, "hT", n0, n)
                    if n0 < 1024:
                        A(lambda e, p=p, n0=n0: e.activation(out=XBt[:, 3 + n0:3 + n0 + 512], in_=p[:], func=AF.Copy), r=[pn],
                          w=[xbn + "%d" % (n0 // 512)])
                    else:
                        A(lambda e, p=p: e.activation(out=XSv_[:, :, 3:11], in_=p[:, 0:128].rearrange("p (s l) -> p s l", l=8),
                                                      func=AF.Copy), r=[pn], w=[xsn])
                wb2, wn2 = load_w_chunk(w_in, 8 + c)
                for (n0, n) in blocks:
                    p, pn = proj(wb2, wn2, hT, "hT", n0, n)
                    A(lambda e, p=p, n0=n0, n=n: e.activation(out=GYt[:, n0:n0 + n], in_=p[:, 0:n], func=AF.Gelu_apprx_tanh),
                      r=[pn], w=[gyn])

            def E_main(c, st):
                XBt, xbn, XSt, xsn, GYt, gyn = st["XB"], st["xbn"], st["XS"], st["xsn"], st["GY"], st["gyn"]
                XSv_ = XSt[:, :].rearrange("p (s l) -> p s l", l=11)
                lru_split(c, hpre[:, c:c + 1], "hpre", XBt, xbn, smp=(XSv_, xsn, stT[:, c, 80:96], "stT"))
                V(lambda e: e.tensor_tensor(out=PRE[:, 0:1024], in0=HB[:, 0:1024], in1=GYt[:, 0:1024], op=ALU.mult),
                  r=["HB0", "HB1", gyn], w=["PRE"])
                V(lambda e: e.tensor_copy(out=hlM[:, c:c + 1], in_=HB[:, 1023:1024]), r=["HB1"], w=["hlM"])
                V(lambda e: e.tensor_copy(out=cvM[:, c, :], in_=XBt[:, 1024:1027]), r=[xbn + "1"], w=["cvM"])
                V(lambda e: e.tensor_tensor(out=PRE[:, 1024:NTOK], in0=SHB[:], in1=GYt[:, 1024:NTOK], op=ALU.mult),
                  r=["SHB", gyn], w=["PRE"])
                V(lambda e: e.tensor_copy(out=hlS[:, c, :], in_=SHB[:].rearrange("p (s l) -> p s l", l=8)[:, :, 7]),
                  r=["SHB"], w=["hlS"])
                V(lambda e: e.tensor_copy(out=cvS[:, c, :].rearrange("p (s k) -> p s k", k=3), in_=XSv_[:, :, 8:11]),
                  r=[xsn], w=["cvS"])
                finish_chunk(c, 0, c == 0)

            P_main(0, bsets[0])
            for c in range(8):
                if c + 1 < 8:
                    P_main(c + 1, bsets[(c + 1) % 2])
                E_main(c, bsets[c % 2])

            S.barrier()
            CXv = XB[:, 0:1026].rearrange("p (s l) -> p s l", s=1)
            CSv = XSB_[:, 0:NSEQ * 10].rearrange("p (s l) -> p s l", l=10)
            for c in range(8):
                wbx, wnx = load_w_chunk(w_in, 32 + c)
                wbc, wnc = load_w_chunk(w_in, 24 + c)
                wbb, wnb = load_w_chunk(w_in, 16 + c)
                V(lambda e, c=c: e.tensor_copy(out=XB[:, 0:2], in_=cxtail[:, c, :]), r=["cxtail"], w=["XB"])
                V(lambda e, c=c: e.tensor_copy(out=CSv[:, :, 0:2], in_=stT[:, c, 48:80].rearrange("p (s k) -> p s k", k=2)),
                  r=["stT"], w=["XSs"])
                for (n0, n) in blocks:
                    px, pxn = proj(wbx, wnx, hT, "hT", n0, n)
                    A(lambda e, px=px, n0=n0, n=n: e.activation(out=GY[:, n0:n0 + n], in_=px[:, 0:n], func=AF.Copy), r=[pxn], w=["GY"])
                    pc, pcn = proj(wbc, wnc, hT, "hT", n0, n)
                    if n0 < 1024:
                        V(lambda e, pc=pc, n0=n0: e.tensor_tensor(out=XB[:, 2 + n0:2 + n0 + 512], in0=pc[:], in1=GY[:, n0:n0 + 512],
                                                                  op=ALU.mult), r=[pcn, "GY"], w=["XB"])
                    else:
                        V(lambda e, pc=pc: e.tensor_tensor(out=CSv[:, :, 2:10], in0=pc[:, 0:128].rearrange("p (s l) -> p s l", l=8),
                                                           in1=GY[:, 1024:NTOK].rearrange("p (s l) -> p s l", l=8), op=ALU.mult),
                          r=[pcn, "GY"], w=["XSs"])
                for (Xv_, res, nseq, L, dst, dres) in ((CXv, "XB", 1, 1024, XC, "XC"), (CSv, "XSs", NSEQ, LS, A2, "A2")):
                    N = nseq * L
                    d3 = dst[:, 0:N].rearrange("p (s l) -> p s l", l=L)
                    V(lambda e, Xv_=Xv_, d3=d3, L=L, c=c: e.tensor_scalar(out=d3, in0=Xv_[:, :, 0:L], scalar1=chv[:, c, 8:9], scalar2=None,
                                                                     op0=ALU.mult), r=[res, "chv"], w=[dres])
                    for k in range(1, 3):
                        V(lambda e, Xv_=Xv_, d3=d3, L=L, k=k, c=c: e.scalar_tensor_tensor(
                            out=d3, in0=Xv_[:, :, k:k + L], scalar=chv[:, c, 8 + k:9 + k], in1=d3, op0=ALU.mult, op1=ALU.add),
                          r=[res, "chv", dres], w=[dres])
                for (n0, n) in blocks:
                    pbk, pbn = proj(wbb, wnb, hT, "hT", n0, n)
                    if n0 < 1024:
                        V(lambda e, pbk=pbk, n0=n0: e.tensor_tensor(out=PRE[:, n0:n0 + 512], in0=pbk[:], in1=XC[:, n0:n0 + 512],
                                                                    op=ALU.mult), r=[pbn, "XC"], w=["PRE"])
                    else:
                        V(lambda e, pbk=pbk: e.tensor_tensor(out=PRE[:, 1024:NTOK], in0=pbk[:, 0:128], in1=A2[:, 0:128], op=ALU.mult),
                          r=[pbn, "A2"], w=["PRE"])
                V(lambda e, c=c: e.tensor_copy(out=scM[:, c, :], in_=XB[:, 1024:1026]), r=["XB"], w=["scM"])
                V(lambda e, c=c: e.tensor_copy(out=scS[:, c, :].rearrange("p (s k) -> p s k", k=2), in_=CSv[:, :, 8:10]),
                  r=["XSs"], w=["scS"])
                finish_chunk(8 + c, 1, c == 0)

            def out_T(src_ap, src_res, nrows, dst_fn):
                for c in range(8):
                    p, pn = pa.next()
                    T(lambda e, p=p, c=c: e.transpose(out=p[0:nrows, 0:128], in_=src_ap(c), identity=ident_f[:]),
                      r=[src_res, "ident_f"], w=[pn])
                    V(lambda e, p=p, c=c: e.tensor_copy(out=SQ[0:nrows, c * 128:(c + 1) * 128], in_=p[0:nrows, 0:128]), r=[pn], w=["SQ"])
                dst_fn()

            out_T(lambda c: cvS[:, c, :], "cvS", 48, lambda: DMA(cvs_d, SQ[0:48, 0:1024], "o_cvs", r=["SQ"]))
            out_T(lambda c: scS[:, c, :], "scS", 32, lambda: DMA(scs_d, SQ[0:32, 0:1024], "o_scs", r=["SQ"]))
            out_T(lambda c: hlS[:, c, :], "hlS", 16, lambda: DMA(hls_d, SQ[0:16, 0:1024], "o_hls", r=["SQ"]))
            out_T(lambda c: cvM[:, c, :], "cvM", 3, lambda: DMA(cvp_d, SQ[0:3, 0:1024], "o_cvp", r=["SQ"]))
            out_T(lambda c: scM[:, c, :], "scM", 2, lambda: DMA(scp_d, SQ[0:2, 0:1024], "o_scp", r=["SQ"]))
            p, pn = pa.next()
            T(lambda e, p=p: e.transpose(out=p[0:8, 0:128], in_=hlM[:, 0:8], identity=ident_f[:]), r=["hlM", "ident_f"], w=[pn])
            V(lambda e, p=p: e.tensor_copy(out=SQ[0:8, 0:128], in_=p[0:8, 0:128]), r=[pn], w=["SQ"])
            DMA(hlp_d, SQ[0:8, 0:128], "o_hlp", r=["SQ"])

            S.barrier()
            mixT = hT
            for g in range(2):
                V(lambda e, g=g: e.tensor_scalar(out=ssq[g][:], in0=ssq[g][:], scalar1=1.0 / 1024, scalar2=EPS, op0=ALU.mult, op1=ALU.add),
                  r=["ssq%d" % g], w=["ssq%d" % g])
                A(lambda e, g=g: e.activation(out=ssq[g][:], in_=ssq[g][:], func=AF.Sqrt), r=["ssq%d" % g], w=["ssq%d" % g])
                V(lambda e, g=g: e.reciprocal(out=ssq[g][:], in_=ssq[g][:]), r=["ssq%d" % g], w=["ssq%d" % g])
            pre_pool = Rot([(PRE, "PRE"), (SQ, "SQ"), (GY, "GY")])
            for cidx in range(16):
                g = cidx // 8
                c = cidx % 8
                bufp, bn = pre_pool.next()
                DMA(bufp[:], pre_d[cidx], bn, r=["pre_d%d" % cidx], w=[bn])
                V(lambda e, bufp=bufp, cidx=cidx, g=g, c=c: e.scalar_tensor_tensor(
                    out=mixT[:, cidx, :], in0=bufp[:], scalar=chv[:, c, 11 + g:12 + g], in1=ssq[g][:], op0=ALU.mult, op1=ALU.mult),
                  r=[bn, "chv", "ssq%d" % g], w=["mixT"])
            if dbg:
                DMA(dbg_mix, mixT[:], "dbg_mix", r=["mixT"])

            S.barrier()
            lsc.__exit__(None, None, None)
            wsc = ExitStack()
            wsc.__enter__()
            tsb = lambda name, shape, dt: wsc.enter_context(nc.sbuf_tensor(uq(name), list(shape), dt))
            modA = [tsb("modA%d" % g, [128, DM], F32) for g in range(2)]
            gb = tsb("gb", [128, DM], F32)
            bada_sb = tsb("bada_sb", [1, 1024], F32)
            wada_pool = Rot([(tsb("wada%d" % i, [128, 1024], BF16), "wada%d" % i) for i in range(5)])
            for g in range(2):
                DMA(modA[g][:], mods_d[0, g], mAn[g], r=["mods_d0"], w=[mAn[g]])
            wo_pool = Rot([(tsb("wo%d" % i, [128, 16, 512], BF16), "wo%d" % i) for i in range(2)])
            xr_pool = Rot([(tsb("xr%d" % i, [128, 512], F32), "xr%d" % i) for i in range(3)])
            wtmp = tsb("wtmp", [128, 512], F32)
            for db in range(4):
                wo, won = wo_pool.next()
                DMA(wo[:], w_out[:, db * 512:(db + 1) * 512].rearrange("(kc p) n -> p kc n", p=128), won, w=[won], eng="gpsimd")
                for i in range(NTILE):
                    g = 0 if i < 8 else 1
                    p, pn = pm.next()
                    for kc in range(16):
                        T(lambda e, p=p, kc=kc, i=i, wo=wo: e.matmul(p[:], lhsT=mixT[:, kc, i * 128:(i + 1) * 128], rhs=wo[:, kc, :],
                                                                     start=(kc == 0), stop=(kc == 15)), r=["mixT", won], w=[pn])
                    xr, xrn = xr_pool.next()
                    DMA(xr[:], xm[i * 128:(i + 1) * 128, db * 512:(db + 1) * 512], xrn, w=[xrn])
                    V(lambda e, p=p, g=g, db=db, xr=xr, mA=modA: e.tensor_tensor(out=wtmp[:], in0=p[:],
                                                                                 in1=mA[g][:, db * 512:(db + 1) * 512], op=ALU.mult),
                      r=[pn, mAn[g]], w=["wtmp"])
                    V(lambda e, xr=xr: e.tensor_tensor(out=xr[:], in0=xr[:], in1=wtmp[:], op=ALU.add),
                      r=[xrn, "wtmp"], w=[xrn])
                    DMA(x1_d[i * 128:(i + 1) * 128, db * 512:(db + 1) * 512], xr[:], xrn, r=[xrn], w=["x1_d%d" % i])

            S.barrier()
            wsc.__exit__(None, None, None)
            n2s = ExitStack()
            n2s.__enter__()
            tsb = lambda name, shape, dt: n2s.enter_context(nc.sbuf_tensor(uq(name), list(shape), dt))
            xt_pool = Rot([(tsb("xt0", [128, DM], F32), "xt0"), (tsb("xt1", [128, DM], F32), "xt1")])
            hb_pool = Rot([(tsb("hb0", [128, DM], BF16), "hb0"), (tsb("hb1", [128, DM], BF16), "hb1")])
            modA = [tsb("modA%d" % g, [128, DM], F32) for g in range(2)]
            modB = [tsb("modB%d" % g, [128, DM], F32) for g in range(2)]
            gb = tsb("gb", [128, DM], F32)
            bada_sb = tsb("bada_sb", [1, 1024], F32)
            wada_pool = Rot([(tsb("wada%d" % i, [128, 1024], BF16), "wada%d" % i) for i in range(5)])
            DMA(gb[:], gvecs[1:2, :].to_broadcast([128, DM]), "gb", w=["gb"])
            for g in range(2):
                DMA(modA[g][:], mods_d[1, g], mAn[g], r=["mods_d1"], w=[mAn[g]])
                DMA(modB[g][:], mods_d[2, g], mBn[g], r=["mods_d2"], w=[mBn[g]])
                V(lambda e, g=g, mA=modA, gb_=gb: e.scalar_tensor_tensor(out=mA[g][:], in0=mA[g][:], scalar=1.0, in1=gb_[:],
                                                                        op0=ALU.add, op1=ALU.mult), r=[mAn[g], "gb"], w=[mAn[g]])
            run_pipelined([lambda i=i: norm_to_T(x1_d[i * 128:(i + 1) * 128, :], 0 if i < 8 else 1, h2T, "h2T", i * 128,
                                                 modA, mAn, modB, mBn, src_reads=["x1_d%d" % i]) for i in range(NTILE)])
            if dbg:
                DMA(dbg_h2T, h2T[:], "dbg_h2T", r=["h2T"])
            S.barrier()
            n2s.__exit__(None, None, None)

        if upto >= 2:
            est = ExitStack()
            with est:
                def esb(name, shape, dt):
                    return est.enter_context(nc.sbuf_tensor(uq(name), list(shape), dt))

                qT = esb("qT", [128, 16, NTOK], BF16)
                PT = esb("PT", [128, 16, NTOK], F32)
                thb = esb("thb", [128, 2, 72], F32)
                wqs = ExitStack()
                wqs.__enter__()
                win_pool = Rot([(wqs.enter_context(nc.sbuf_tensor(uq("wq%d" % i), [128, 16, 128], BF16)), "wq%d" % i) for i in range(2)])
                for hp in range(16):
                    wb, wn = win_pool.next()
                    DMA(wb[:], wq[:, hp * 128:(hp + 1) * 128].rearrange("(kc p) n -> p kc n", p=128), wn, w=[wn], eng="gpsimd")
                    for tt in range(3):
                        p, pn = pm.next()
                        for kc in range(16):
                            T(lambda e, p=p, kc=kc, tt=tt, wb=wb: e.matmul(p[:, 0:TT], lhsT=wb[:, kc, :], rhs=h2T[:, kc, tt * TT:(tt + 1) * TT],
                                                                           start=(kc == 0), stop=(kc == 15)), r=[wn, "h2T"], w=[pn])
                        A(lambda e, p=p, hp=hp, tt=tt: e.activation(out=qT[:, hp, tt * TT:(tt + 1) * TT], in_=p[:, 0:TT], func=AF.Copy),
                          r=[pn], w=["qT"])
                if dbg:
                    DMA(dbg_qT, qT[:], "dbg_qT", r=["qT"])
                S.barrier()
                wqs.__exit__(None, None, None)
                tst2 = ExitStack()
                with tst2:
                    def t2(name, shape, dt):
                        return tst2.enter_context(nc.sbuf_tensor(uq(name), list(shape), dt))
                    vtop = t2("vtop", [128, 16, 24], F32)
                    scs = t2("scs", [128, 16, 128], F32)
                    wk4 = t2("wk4", [128, 4, 128], F32)
                    cand = t2("cand", [128, 8, 289], F32)
                    wk2 = t2("wk2", [128, 4, 289], F32)
                    ctop = t2("ctop", [128, 8, 24], F32)
                    dlt = t2("dlt", [128, 8, 16], F32)
                    zz = t2("zz", [128, 2, 8], F32)
                    AX = mybir.AxisListType
                    for i in range(NTILE):
                        banks = [pm.next() for _ in range(4)]
                        for hp in range(16):
                            p, pn = banks[hp // 4]
                            o = (hp % 4) * 128
                            T(lambda e, p=p, o=o, hp=hp, i=i: e.matmul(p[:, o:o + 128], lhsT=qT[:, hp, i * 128:(i + 1) * 128],
                                                                       rhs=KT[:, hp % 2, :], start=True, stop=True),
                              r=["qT", "KT"], w=[pn])
                        for b_ in range(4):
                            p, pn = banks[b_]
                            A(lambda e, p=p, b_=b_: e.activation(out=scs[:, b_ * 4:(b_ + 1) * 4, :],
                                                                 in_=p[:].rearrange("p (q n) -> p q n", n=128), func=AF.Copy),
                              r=[pn], w=["scs%d" % b_])
                        for b_ in range(4):
                            hps = [b_ * 4 + j for j in range(4)]
                            sr = "scs%d" % b_
                            for j, hp in enumerate(hps):
                                V(lambda e, hp=hp: e.max(out=vtop[:, hp, 0:8], in_=scs[:, hp, :]), r=[sr], w=["vtop%d" % hp])
                            for j, hp in enumerate(hps):
                                V(lambda e, hp=hp, j=j: e.match_replace(out=wk4[:, j, :], in_to_replace=vtop[:, hp, 0:8], in_values=scs[:, hp, :],
                                                                        imm_value=-1e30), r=[sr, "vtop%d" % hp], w=["wk4_%d" % j])
                            for j, hp in enumerate(hps):
                                V(lambda e, hp=hp, j=j: e.max(out=vtop[:, hp, 8:16], in_=wk4[:, j, :]), r=["wk4_%d" % j], w=["vtop%d" % hp])
                            for j, hp in enumerate(hps):
                                V(lambda e, hp=hp, j=j: e.match_replace(out=wk4[:, j, :], in_to_replace=vtop[:, hp, 8:16], in_values=wk4[:, j, :],
                                                                        imm_value=-1e30), r=["wk4_%d" % j, "vtop%d" % hp], w=["wk4_%d" % j])
                            for j, hp in enumerate(hps):
                                V(lambda e, hp=hp, j=j: e.max(out=vtop[:, hp, 16:24], in_=wk4[:, j, :]), r=["wk4_%d" % j], w=["vtop%d" % hp])
                        vt4 = vtop[:].rearrange("q (h p) k -> q h p k", p=2)
                        V(lambda e, vt4=vt4: e.tensor_tensor(
                            out=cand[:].rearrange("q h (a b) -> q h a b", b=17),
                            in0=vt4[:, :, 0, 0:17].unsqueeze(3).to_broadcast([128, 8, 17, 17]),
                            in1=vt4[:, :, 1, 0:17].unsqueeze(2).to_broadcast([128, 8, 17, 17]), op=ALU.add),
                          r=["vtop%d" % hp for hp in range(16)], w=["cand"])
                        for b_ in range(2):
                            hs = [b_ * 4 + j for j in range(4)]
                            for j, h in enumerate(hs):
                                V(lambda e, h=h: e.max(out=ctop[:, h, 0:8], in_=cand[:, h, :]), r=["cand"], w=["ctop%d" % h])
                            for j, h in enumerate(hs):
                                V(lambda e, h=h, j=j: e.match_replace(out=wk2[:, j, :], in_to_replace=ctop[:, h, 0:8], in_values=cand[:, h, :],
                                                                      imm_value=-1e30), r=["cand", "ctop%d" % h], w=["wk2_%d" % j])
                            for j, h in enumerate(hs):
                                V(lambda e, h=h, j=j: e.max(out=ctop[:, h, 8:16], in_=wk2[:, j, :]), r=["wk2_%d" % j], w=["ctop%d" % h])
                            for j, h in enumerate(hs):
                                V(lambda e, h=h, j=j: e.match_replace(out=wk2[:, j, :], in_to_replace=ctop[:, h, 8:16], in_values=wk2[:, j, :],
                                                                      imm_value=-1e30), r=["wk2_%d" % j, "ctop%d" % h], w=["wk2_%d" % j])
                            for j, h in enumerate(hs):
                                V(lambda e, h=h, j=j: e.max(out=ctop[:, h, 16:24], in_=wk2[:, j, :]), r=["wk2_%d" % j], w=["ctop%d" % h])
                        call = ["ctop%d" % h for h in range(8)]
                        i8 = slice(i * 8, (i + 1) * 8)
                        V(lambda e, i8=i8: e.tensor_tensor(out=thb[:, 0, i8], in0=ctop[:, :, 15], in1=ctop[:, :, 16], op=ALU.add),
                          r=call, w=["thb"])
                        V(lambda e, i8=i8: e.tensor_scalar(out=thb[:, 0, i8], in0=thb[:, 0, i8], scalar1=0.5, scalar2=None, op0=ALU.mult),
                          r=["thb"], w=["thb"])
                        V(lambda e: e.tensor_tensor(out=dlt[:], in0=ctop[:, :, 0:16], in1=ctop[:, :, 0:1].to_broadcast([128, 8, 16]),
                                                    op=ALU.subtract), r=call, w=["dlt"])
                        A(lambda e: e.activation(out=dlt[:], in_=dlt[:], func=AF.Exp), r=["dlt"], w=["dlt"])
                        V(lambda e: e.tensor_reduce(out=zz[:, 0, :], in_=dlt[:], axis=AX.X, op=ALU.add), r=["dlt"], w=["zz"])
                        A(lambda e: e.activation(out=zz[:, 1, :], in_=zz[:, 0, :], func=AF.Ln), r=["zz"], w=["zz"])
                        V(lambda e, i8=i8: e.scalar_tensor_tensor(out=thb[:, 1, i8], in0=ctop[:, :, 0], scalar=-1.0, in1=zz[:, 1, :],
                                                                  op0=ALU.mult, op1=ALU.subtract), r=call + ["zz"], w=["thb"])
                    if dbg:
                        DMA(dbg_th, thb[:], "dbg_th", r=["thb"])
                    S.barrier()

                if upto >= 3:
                    psD = Rot([(pf[0], "pf0"), (pf[1], "pf1"), (pf[4], "pf4"), (pf[5], "pf5")])
                    psG = Rot([(pf[2], "pf2"), (pf[3], "pf3")])
                    psS = Rot([(pf[3], "pf3"), (pf[4], "pf4"), (pf[5], "pf5")])
                    psO = Rot([(pf[4], "pf4"), (pf[5], "pf5")])
                    lst = ExitStack()
                    lst.__enter__()
                    esb_outer = esb
                    esb = lambda name, shape, dt: lst.enter_context(nc.sbuf_tensor(uq(name), list(shape), dt))
                    UTp = Rot([(esb("UT%d" % i, [128, 16, 128], BF16), "UT%d" % i) for i in range(4)])
                    Vb = esb("Vb", [128, GCH, DM], BF16)
                    AT = esb("AT", [128, GCH, NTOK], BF16)
                    Ebuf = Rot([(esb("Eb%d" % i, [128, 512], BF16), "Eb%d" % i) for i in range(4)])
                    Gbuf = Rot([(esb("Gb%d" % i, [128, 512], BF16), "Gb%d" % i) for i in range(4)])
                    Kx1 = Rot([(esb("Kx1_%d" % i, [128, GCH, 128], BF16), "Kx1_%d" % i) for i in range(2)])
                    Kx2 = esb("Kx2", [128, GCH, 128], BF16)
                    V(lambda e: e.tensor_copy(out=Kx2[:], in_=KT[:, 1, :].unsqueeze(1).to_broadcast([128, GCH, 128])), r=["KT"], w=["Kx2"])
                    utslots = {}

                    def load_u(n):
                        ut_, utn_ = UTp.next()
                        DMA(ut_[:], puT[:, n * 128:(n + 1) * 128].rearrange("(dc p) e -> p dc e", p=128), utn_, w=[utn_], eng="gpsimd")
                        utslots[n] = (ut_, utn_)

                    def do_T(n):
                        pass

                    load_u(0)
                    load_u(1)
                    load_u(2)
                    for grp in range(NGRP):
                        k1, k1n = Kx1.next()
                        V(lambda e, k1=k1, grp=grp: e.tensor_copy(
                            out=k1[:], in_=KT[:, 0, grp * GCH:(grp + 1) * GCH].unsqueeze(2).to_broadcast([128, GCH, 128])),
                          r=["KT"], w=[k1n])
                        for cc in range(GCH):
                            ech = grp * GCH + cc
                            DMA(Vb[:, cc, :], pv[ech * 128:(ech + 1) * 128, :], "Vb%d" % cc, w=["Vb%d" % cc], eng="gpsimd")
                        for cc in range(GCH):
                            n = grp * GCH + cc
                            if n + 3 < 128:
                                load_u(n + 3)
                            ut, utn = utslots.pop(n)
                            for tt in range(3):
                                p, pn = psS.next()
                                for dc in range(16):
                                    T(lambda e, p=p, dc=dc, tt=tt, ut=ut: e.matmul(p[:, 0:TT], lhsT=ut[:, dc, :], rhs=h2T[:, dc, tt * TT:(tt + 1) * TT],
                                                                                   start=(dc == 0), stop=(dc == 15)), r=[utn, "h2T"], w=[pn])
                                A(lambda e, p=p, cc=cc, tt=tt: e.activation(out=AT[:, cc, tt * TT:(tt + 1) * TT], in_=p[:, 0:TT],
                                                                            func=AF.Gelu_apprx_tanh), r=[pn], w=["AT"])
                        LAG = 3
                        pgs = {}

                        def issue_gt(i, h, gbf, gbn):
                            if h == 0:
                                pgs[i] = psG.next()
                            pg, pgn = pgs[i]
                            for cc in range(GCH):
                                T(lambda e, pg=pg, gbf=gbf, cc=cc, h=h: e.matmul(
                                    pg[:, cc * 128:(cc + 1) * 128], lhsT=gbf[:, cc * 128:(cc + 1) * 128], rhs=ident_b[:],
                                    start=(h == 0 and cc == 0), stop=(h == 7 and cc == GCH - 1), skip_group_check=True),
                                  r=[gbn, "ident_b"], w=[pgn])
                            if h == 7:
                                V(lambda e, pg=pg, i=i: e.tensor_tensor(out=AT[:, :, i * 128:(i + 1) * 128],
                                                                        in0=pg[:].rearrange("p (c t) -> p c t", t=128),
                                                                        in1=AT[:, :, i * 128:(i + 1) * 128], op=ALU.mult),
                                  r=[pgn, "AT"], w=["AT"])

                        pend = []
                        for i in range(NTILE):
                            for h in range(8):
                                idx = i * 8 + h
                                pd, pdn = psD.next()
                                eb, ebn = Ebuf.next()
                                gbf, gbn = Gbuf.next()
                                T(lambda e, pd=pd, h=h, i=i, k1=k1: e.matmul(pd[:], lhsT=qT[:, 2 * h, i * 128:(i + 1) * 128],
                                                                             rhs=k1[:].rearrange("p a b -> p (a b)"), start=True, stop=False),
                                  r=["qT", k1n], w=[pdn])
                                T(lambda e, pd=pd, h=h, i=i: e.matmul(pd[:], lhsT=qT[:, 2 * h + 1, i * 128:(i + 1) * 128],
                                                                      rhs=Kx2[:].rearrange("p a b -> p (a b)"), start=False, stop=True),
                                  r=["qT", "Kx2"], w=[pdn])
                                A(lambda e, pd=pd, eb=eb, idx=idx: e.activation(out=eb[:], in_=pd[:], func=AF.Exp, bias=thb[:, 1, idx:idx + 1]),
                                  r=[pdn, "thb"], w=[ebn])
                                V(lambda e, pd=pd, eb=eb, gbf=gbf, idx=idx: e.scalar_tensor_tensor(
                                    out=gbf[:], in0=pd[:], scalar=thb[:, 0, idx:idx + 1], in1=eb[:], op0=ALU.is_ge, op1=ALU.mult),
                                  r=[pdn, "thb", ebn], w=[gbn])
                                pend.append((i, h, gbf, gbn))
                                if len(pend) > LAG:
                                    issue_gt(*pend.pop(0))
                        while pend:
                            issue_gt(*pend.pop(0))
                        for dc in range(16):
                            for tt in range(3):
                                po, pon = psO.next()
                                for cc in range(GCH):
                                    T(lambda e, po=po, cc=cc, dc=dc, tt=tt: e.matmul(
                                        po[:, 0:TT], lhsT=Vb[:, cc, dc * 128:(dc + 1) * 128], rhs=AT[:, cc, tt * TT:(tt + 1) * TT],
                                        start=(cc == 0), stop=(cc == GCH - 1)), r=["Vb%d" % cc, "AT"], w=[pon])
                                if grp == 0:
                                    V(lambda e, po=po, dc=dc, tt=tt: e.tensor_copy(out=PT[:, dc, tt * TT:(tt + 1) * TT], in_=po[:, 0:TT]),
                                      r=[pon], w=["PT"])
                                else:
                                    V(lambda e, po=po, dc=dc, tt=tt: e.tensor_tensor(out=PT[:, dc, tt * TT:(tt + 1) * TT], in0=po[:, 0:TT],
                                                                                     in1=PT[:, dc, tt * TT:(tt + 1) * TT], op=ALU.add),
                                      r=[pon, "PT"], w=["PT"])
                    if dbg:
                        DMA(dbg_PT, PT[:], "dbg_PT", r=["PT"])
                    S.barrier()
                    lst.__exit__(None, None, None)
                    fst = ExitStack()
                    with fst:
                        def fsb(name, shape, dt):
                            return fst.enter_context(nc.sbuf_tensor(uq(name), list(shape), dt))
                        g2 = [fsb("g2_%d" % g, [128, DM], F32) for g in range(2)]
                        fgb = fsb("fgb", [128, DM], F32)
                        xf_pool = Rot([(fsb("xf%d" % i, [128, DM], F32), "xf%d" % i) for i in range(2)])
                        tmpf = Rot([(fsb("tmpf%d" % i, [128, 512], F32), "tmpf%d" % i) for i in range(2)])
                        junkf = fsb("junkf", [128, DM], BF16)
                        for g in range(2):
                            DMA(g2[g][:], mods_d[3, g], "g2_%d" % g, r=["mods_d3"], w=["g2_%d" % g])
                        DMA(fgb[:], gvecs[2:3, :].to_broadcast([128, DM]), "fgb", w=["fgb"])
                        for i in range(NTILE):
                            g = 0 if i < 8 else 1
                            xf, xfn = xf_pool.next()
                            DMA(xf[:], x1_d[i * 128:(i + 1) * 128, :], xfn, r=["x1_d%d" % i], w=[xfn])
                            for db in range(4):
                                p, pn = pm.next()
                                for j in range(4):
                                    T(lambda e, p=p, j=j, db=db, i=i: e.transpose(out=p[:, j * 128:(j + 1) * 128],
                                                                                  in_=PT[:, db * 4 + j, i * 128:(i + 1) * 128],
                                                                                  identity=ident_f[:]),
                                      r=["PT", "ident_f"], w=[pn])
                                tf, tfn = tmpf.next()
                                V(lambda e, p=p, tf=tf, g=g, db=db: e.tensor_tensor(out=tf[:], in0=p[:], in1=g2[g][:, db * 512:(db + 1) * 512],
                                                                                    op=ALU.mult), r=[pn, "g2_%d" % g], w=[tfn])
                                V(lambda e, tf=tf, xf=xf, db=db: e.tensor_tensor(out=xf[:, db * 512:(db + 1) * 512], in0=xf[:, db * 512:(db + 1) * 512],
                                                                                 in1=tf[:], op=ALU.add), r=[tfn, xfn], w=[xfn])
                            si = stat_i[0] % 64
                            stat_i[0] += 1
                            sres = "stat%d" % si
                            A(lambda e, xf=xf, si=si: e.activation(out=junkf[:], in_=xf[:], func=AF.Square, accum_out=stats[:, si, 0:1]),
                              r=[xfn], w=["junkf", sres])
                            A(lambda e, si=si: e.activation(out=stats[:, si, 1:2], in_=stats[:, si, 0:1], func=AF.Sqrt, scale=1.0 / DM, bias=EPS),
                              r=[sres], w=[sres])
                            V(lambda e, si=si: e.reciprocal(out=stats[:, si, 2:3], in_=stats[:, si, 1:2]), r=[sres], w=[sres])
                            V(lambda e, xf=xf, si=si: e.scalar_tensor_tensor(out=xf[:], in0=xf[:], scalar=stats[:, si, 2:3], in1=fgb[:],
                                                                             op0=ALU.mult, op1=ALU.mult), r=[xfn, sres, "fgb"], w=[xfn])
                            DMA(y_d[i * 128:(i + 1) * 128, :], xf[:], xfn, r=[xfn], w=["y_out"])
        S.emit(gst)
    return nc


def make_in_maps(inp):
    f = lambda a: np.ascontiguousarray(np.asarray(a, dtype=np.float32))
    xp = f(inp["x_prompt"])
    xs = f(inp["x_sample"])
    cp = f(inp["c_prompt"])
    cs = f(inp["c_sample"])
    chvec = np.concatenate([f(inp["lru_conv_w"])[0], f(inp["lru_conv_b"]), f(inp["lru_ba"]), f(inp["lru_bx"]),
                            f(inp["lru_lambda"]), f(inp["sconv_w"])[0], f(inp["gnorm_lru_g"]), f(inp["gnorm_sc_g"])], axis=0)
    gvecs = np.stack([f(inp["norm1_g"])[0], f(inp["norm2_g"])[0], f(inp["final_g"])], axis=0)
    shared = {
        "w_ada": f(inp["w_ada"])[0], "b_ada": f(inp["b_ada"]), "gvecs": f(gvecs), "w_in": f(inp["w_in"])[0],
        "chvec": f(chvec), "lru_wa": f(inp["lru_wa"])[0], "lru_wx": f(inp["lru_wx"])[0], "w_out": f(inp["w_out"])[0],
        "wq": f(inp["peer_wq"])[0], "keys": f(inp["peer_sub_keys"])[0].reshape(256, 128),
        "puT": np.ascontiguousarray(f(inp["peer_u"])[0].T), "pv": f(inp["peer_v"])[0],
    }
    maps = []
    for k in range(8):
        b, hh = k // 2, k % 2
        sl = slice(16 * k, 16 * (k + 1))
        m = dict(shared)
        m["xm"] = f(np.concatenate([xp[b, hh * 1024:(hh + 1) * 1024], xs[sl].reshape(128, DM)], axis=0))
        m["xpre"] = f(xp[b, 0:1024])
        m["ctok"] = f(np.concatenate([np.repeat(cp[b:b + 1], 128, axis=0), np.repeat(cs[sl], 8, axis=0)], axis=0))
        m["flag"] = np.full((128, 1), float(hh), np.float32)
        m["st_h"] = f(inp["state_lru_h"][0][sl])
        m["st_conv"] = f(inp["state_lru_conv"][0][sl].reshape(48, 1024))
        m["st_sc"] = f(inp["state_sconv"][0][sl].reshape(32, 1024))
        maps.append(m)
    return maps


_NC_CACHE = {}


def kernel(**inputs):
    if "nc" not in _NC_CACHE:
        _NC_CACHE["nc"] = build_program()
    nc = _NC_CACHE["nc"]
    maps = make_in_maps(inputs)
    res = run_bass_kernel_spmd(nc, maps, core_ids=list(range(8)))
    R = res.results
    y_p = np.zeros((4, 2048, DM), np.float32)
    y_s = np.zeros((128, 8, DM), np.float32)
    hp = np.zeros((1, 4, 1024), np.float32)
    cvp = np.zeros((1, 4, 3, 1024), np.float32)
    scp = np.zeros((1, 4, 2, 1024), np.float32)
    hs = np.zeros((1, 128, 1024), np.float32)
    cvs = np.zeros((1, 128, 3, 1024), np.float32)
    scs = np.zeros((1, 128, 2, 1024), np.float32)
    for k in range(8):
        b, hh = k // 2, k % 2
        r = R[k]
        y_p[b, hh * 1024:(hh + 1) * 1024] = r["y"][0:1024]
        y_s[16 * k:16 * (k + 1)] = r["y"][1024:].reshape(16, 8, DM)
        if hh == 1:
            hp[0, b] = r["hl_p"].reshape(1024)
            cvp[0, b] = r["cv_p"]
            scp[0, b] = r["sc_p"]
        hs[0, 16 * k:16 * (k + 1)] = r["hl_s"]
        cvs[0, 16 * k:16 * (k + 1)] = r["cv_s"].reshape(16, 3, 1024)
        scs[0, 16 * k:16 * (k + 1)] = r["sc_s"].reshape(16, 2, 1024)
    return (y_p, y_s, hp, cvp, scp, hs, cvs, scs)
```

```python
import numpy as np
from contextlib import ExitStack
import concourse.bass as bass
import concourse.mybir as mybir
from concourse.bass_utils import run_bass_kernel_spmd

F32 = mybir.dt.float32
BF16 = mybir.dt.bfloat16
AF = mybir.ActivationFunctionType
ALU = mybir.AluOpType

DM = 2048
NMAIN = 1024
NSMP = 128
NTOK = NMAIN + NSMP
NPRE = 1024
NSEQ = 16
LS = 8
EPS = 1e-6
NEXP = 16384
GCH = 4
NGRP = 128 // GCH
NTILE = NTOK // 128
TT = 384


class Sched:
    ENGS = ("tensor", "vector", "scalar", "gpsimd", "sync")
    EPOCH = 30000

    def __init__(self, nc):
        self.nc = nc
        self.ops = []
        self.eng_count = {e: 0 for e in self.ENGS}
        self.last_writer = {}
        self.readers = {}
        self.dma_cum = {}
        self.waited = {e: {} for e in self.ENGS}
        self.semnames = []
        self.barrier_toks = None

    def _sem(self, name):
        if name not in self.semnames:
            self.semnames.append(name)

    def _deps(self, reads, writes):
        deps = set()
        for r in reads:
            w = self.last_writer.get(r)
            if w is not None:
                deps.add(w)
        for w_ in writes:
            w = self.last_writer.get(w_)
            if w is not None:
                deps.add(w)
            for rd in self.readers.get(w_, ()):
                deps.add(rd)
        if self.barrier_toks is not None:
            deps.update(self.barrier_toks)
        return deps

    def _record(self, tok, reads, writes):
        for r in reads:
            self.readers.setdefault(r, []).append(tok)
        for w_ in writes:
            self.last_writer[w_] = tok
            self.readers[w_] = []

    def _waits_for(self, eng, deps):
        waits = {}
        for (semname, val) in deps:
            if eng == "tensor" and semname.startswith("e_tensor"):
                continue
            if self.waited[eng].get(semname, 0) >= val:
                continue
            if waits.get(semname, 0) < val:
                waits[semname] = val
        for s, v in waits.items():
            self.waited[eng][s] = v
        return waits

    def op(self, eng, fn, reads=(), writes=()):
        deps = self._deps(reads, writes)
        waits = self._waits_for(eng, deps)
        n = self.eng_count[eng]
        self.eng_count[eng] = n + 1
        semname = "e_%s_%d" % (eng, n // self.EPOCH)
        self._sem(semname)
        tok = (semname, n % self.EPOCH + 1)
        self.ops.append((eng, fn, waits, (semname, 1)))
        self._record(tok, reads, writes)
        return tok

    def dma(self, eng, fn, key, reads=(), writes=()):
        deps = self._deps(reads, writes)
        waits = self._waits_for(eng, deps)
        semname = "d_" + key
        self._sem(semname)
        self.dma_cum[semname] = self.dma_cum.get(semname, 0) + 16
        tok = (semname, self.dma_cum[semname])
        self.ops.append((eng, fn, waits, (semname, 16)))
        self._record(tok, reads, writes)
        return tok

    def _all_toks(self):
        toks = set()
        for e in self.ENGS:
            n = self.eng_count[e]
            if n:
                toks.add(("e_%s_%d" % (e, (n - 1) // self.EPOCH), (n - 1) % self.EPOCH + 1))
        for s, v in self.dma_cum.items():
            toks.add((s, v))
        return toks

    def barrier(self):
        self.barrier_toks = self._all_toks()

    def emit(self, st):
        nc = self.nc
        sems = {}
        for s in self.semnames:
            sems[s] = st.enter_context(nc.semaphore(s))
        final = self._all_toks()
        block = st.enter_context(nc.Block())
        per_eng = {e: [] for e in self.ENGS}
        for (eng, fn, waits, inc) in self.ops:
            per_eng[eng].append((fn, waits, inc))

        def make(eng_name):
            def body(engobj):
                for (fn, waits, inc) in per_eng[eng_name]:
                    for s, v in waits.items():
                        engobj.wait_ge(sems[s], v)
                    ins = fn(engobj)
                    ins.then_inc(sems[inc[0]], inc[1])
                if eng_name == "sync":
                    for (s, v) in final:
                        engobj.wait_ge(sems[s], v)
            return body

        for e in self.ENGS:
            if per_eng[e] or e == "sync":
                getattr(block, e)(make(e))


class Rot:
    def __init__(self, items):
        self.items = items
        self.i = 0

    def next(self):
        it = self.items[self.i % len(self.items)]
        self.i += 1
        return it


def build_program(dbg=False, upto=99):
    nc = bass.Bass("TRN2", target_bir_lowering=False)

    def din(name, shape, dt=F32):
        return nc.dram_tensor(name, list(shape), dt, kind="ExternalInput").ap()

    def dout(name, shape, dt=F32):
        return nc.dram_tensor(name, list(shape), dt, kind="ExternalOutput").ap()

    def dscr(name, shape, dt=F32):
        return nc.dram_tensor(name, list(shape), dt, kind="ExternalOutput" if dbg else "Internal").ap()

    xm = din("xm", [NTOK, DM])
    xpre = din("xpre", [NPRE, DM])
    ctok = din("ctok", [256, DM])
    flag_d = din("flag", [128, 1])
    st_h = din("st_h", [NSEQ, 1024])
    st_conv = din("st_conv", [NSEQ * 3, 1024])
    st_sc = din("st_sc", [NSEQ * 2, 1024])
    w_ada = din("w_ada", [DM, 6 * DM])
    b_ada = din("b_ada", [1, 6 * DM])
    gvecs = din("gvecs", [3, DM])
    w_in = din("w_in", [DM, 5120])
    chvec = din("chvec", [13, 1024])
    lru_wa = din("lru_wa", [16, 64, 64])
    lru_wx = din("lru_wx", [16, 64, 64])
    w_out = din("w_out", [DM, DM])
    wq = din("wq", [DM, DM])
    keys = din("keys", [256, 128])
    puT = din("puT", [DM, NEXP])
    pv = din("pv", [NEXP, DM])

    y_d = dout("y", [NTOK, DM])
    hlp_d = dout("hl_p", [8, 128])
    cvp_d = dout("cv_p", [3, 1024])
    scp_d = dout("sc_p", [2, 1024])
    hls_d = dout("hl_s", [NSEQ, 1024])
    cvs_d = dout("cv_s", [NSEQ * 3, 1024])
    scs_d = dout("sc_s", [NSEQ * 2, 1024])

    pre_d = dscr("pre_d", [16, 128, NTOK])
    x1_d = dscr("x1_d", [NTOK, DM])
    mods_d = dscr("mods_d", [4, 2, 128, DM])
    if dbg:
        dbg_hT = dout("dbg_hT", [128, 16, NTOK], BF16)
        dbg_mix = dout("dbg_mix", [128, 16, NTOK], BF16)
        dbg_h2T = dout("dbg_h2T", [128, 16, NTOK], BF16)
        dbg_qT = dout("dbg_qT", [128, 16, NTOK], BF16)
        dbg_th = dout("dbg_th", [128, 2, 72])
        dbg_PT = dout("dbg_PT", [128, 16, NTOK])

    _uid = [0]

    def uq(name):
        _uid[0] += 1
        return "%s_%d" % (name, _uid[0])

    S = Sched(nc)
    V = lambda fn, r=(), w=(): S.op("vector", fn, r, w)
    A = lambda fn, r=(), w=(): S.op("scalar", fn, r, w)
    G = lambda fn, r=(), w=(): S.op("gpsimd", fn, r, w)
    T = lambda fn, r=(), w=(): S.op("tensor", fn, r, w)

    def DMA(out, in_, key, r=(), w=(), eng="sync"):
        S.dma(eng, lambda e: e.dma_start(out=out, in_=in_), key, r, w)

    gst = ExitStack()
    with gst:
        def gsb(name, shape, dt):
            return gst.enter_context(nc.sbuf_tensor(uq(name), list(shape), dt))

        pf = [gst.enter_context(nc.psum_tensor("pf%d" % i, [128, 512], F32)) for i in range(6)]
        pb = [gst.enter_context(nc.psum_tensor("pb%d" % i, [128, 1024], BF16)) for i in range(2)]
        pm = Rot([(pf[i], "pf%d" % i) for i in range(4)])
        pa = Rot([(pf[4], "pf4"), (pf[5], "pf5")])
        pt = Rot([(pb[0], "pb0"), (pb[1], "pb1")])

        ident_f = gsb("ident_f", [128, 128], F32)
        ident_b = gsb("ident_b", [128, 128], BF16)
        ones_f = gsb("ones_f", [128, 128], F32)
        chv = gsb("chv", [128, 8, 16], F32)
        flag = gsb("flag_sb", [128, 1], F32)
        h2T = gsb("h2T", [128, 16, NTOK], BF16)
        KT = gsb("KT", [128, 2, 128], BF16)
        stats = gsb("stats", [128, 64, 4], F32)
        stat_i = [0]

        G(lambda e: e.memset(ident_f[:], 0.0), w=["ident_f"])
        G(lambda e: e.affine_select(out=ident_f[:], in_=ident_f[:], pattern=[[-1, 128]],
                                    compare_op=ALU.not_equal, fill=1.0, base=0, channel_multiplier=1),
          r=["ident_f"], w=["ident_f"])
        V(lambda e: e.tensor_copy(out=ident_b[:], in_=ident_f[:]), r=["ident_f"], w=["ident_b"])
        V(lambda e: e.memset(ones_f[:], 1.0), w=["ones_f"])
        DMA(flag[:], flag_d, "flag", w=["flag"])

        tst = ExitStack()
        with tst:
            def tsb(name, shape, dt):
                return tst.enter_context(nc.sbuf_tensor(uq(name), list(shape), dt))

            BDa = tsb("BDa", [128, 8, 128], BF16)
            BDx = tsb("BDx", [128, 8, 128], BF16)
            stT = tsb("stT", [128, 8, 96], F32)
            cT = tsb("cT", [128, 2, 16, 128], BF16)
            hT = tsb("hT", [128, 16, NTOK], BF16)
            hTp = tsb("hTp", [128, 16, NPRE], BF16)
            sst = ExitStack()
            sst.__enter__()
            tsb_outer = tsb
            tsb = lambda name, shape, dt: sst.enter_context(nc.sbuf_tensor(uq(name), list(shape), dt))
            chv_tm = tsb("chv_tm", [13, 1024], F32)
            DMA(chv_tm[:], chvec, "chv_tm", w=["chv_tm"])
            for c in range(8):
                p, pn = pa.next()
                T(lambda e, p=p, c=c: e.transpose(out=p[:, 0:13], in_=chv_tm[0:13, c * 128:(c + 1) * 128],
                                                  identity=ident_f[0:13, 0:13]),
                  r=["chv_tm", "ident_f"], w=[pn])
                V(lambda e, p=p, c=c: e.tensor_copy(out=chv[:, c, 0:13], in_=p[:, 0:13]), r=[pn], w=["chv"])
            A(lambda e: e.activation(out=chv[:, :, 13], in_=chv[:, :, 7], func=AF.Exp, scale=-1.0), r=["chv"], w=["chv"])
            A(lambda e: e.activation(out=chv[:, :, 13], in_=chv[:, :, 13], func=AF.Ln, bias=1.0), r=["chv"], w=["chv"])
            V(lambda e: e.tensor_scalar(out=chv[:, :, 14], in0=chv[:, :, 13], scalar1=-16.0, scalar2=None, op0=ALU.mult),
              r=["chv"], w=["chv"])
            V(lambda e: e.tensor_scalar(out=chv[:, :, 13], in0=chv[:, :, 13], scalar1=-8.0, scalar2=None, op0=ALU.mult),
              r=["chv"], w=["chv"])
            keys_tm = tsb("keys_tm", [128, 2, 128], F32)
            for p_ in range(2):
                DMA(keys_tm[:, p_, :], keys[p_ * 128:(p_ + 1) * 128, :], "keys_tm", w=["keys_tm"])
            for p_ in range(2):
                p, pn = pa.next()
                T(lambda e, p=p, p_=p_: e.transpose(out=p[:, 0:128], in_=keys_tm[:, p_, :], identity=ident_f[:]),
                  r=["keys_tm", "ident_f"], w=[pn])
                V(lambda e, p=p, p_=p_: e.tensor_copy(out=KT[:, p_, :], in_=p[:, 0:128]), r=[pn], w=["KT"])
            st_tm = tsb("st_tm", [96, 1024], F32)
            DMA(st_tm[0:48, :], st_conv, "st_tm", w=["st_tm"])
            DMA(st_tm[48:80, :], st_sc, "st_tm", w=["st_tm"])
            DMA(st_tm[80:96, :], st_h, "st_tm", w=["st_tm"])
            for c in range(8):
                p, pn = pa.next()
                T(lambda e, p=p, c=c: e.transpose(out=p[:, 0:96], in_=st_tm[0:96, c * 128:(c + 1) * 128],
                                                  identity=ident_f[0:96, 0:96]),
                  r=["st_tm", "ident_f"], w=[pn])
                V(lambda e, p=p, c=c: e.tensor_copy(out=stT[:, c, :], in_=p[:, 0:96]), r=[pn], w=["stT"])

            S.barrier()
            sst.__exit__(None, None, None)
            nst = ExitStack()
            nst.__enter__()
            tsb = lambda name, shape, dt: nst.enter_context(nc.sbuf_tensor(uq(name), list(shape), dt))
            xt_pool = Rot([(tsb("xt0", [128, DM], F32), "xt0"), (tsb("xt1", [128, DM], F32), "xt1")])
            hb_pool = Rot([(tsb("hb0", [128, DM], BF16), "hb0"), (tsb("hb1", [128, DM], BF16), "hb1")])

            def transpose16(src_bf, src_res, dst_fn, dst_res):
                for b in range(2):
                    p, pn = pt.next()
                    for j in range(8):
                        dc = b * 8 + j
                        T(lambda e, p=p, j=j, dc=dc: e.transpose(out=p[:, j * 128:(j + 1) * 128],
                                                                 in_=src_bf[:, dc * 128:(dc + 1) * 128],
                                                                 identity=ident_b[:]),
                          r=[src_res, "ident_b"], w=[pn])
                    if b == 0:
                        A(lambda e, p=p, b=b: e.activation(out=dst_fn(b), in_=p[:].rearrange("p (j t) -> p j t", t=128),
                                                           func=AF.Copy), r=[pn], w=[dst_res])
                    else:
                        V(lambda e, p=p, b=b: e.tensor_copy(out=dst_fn(b), in_=p[:].rearrange("p (j t) -> p j t", t=128)),
                          r=[pn], w=[dst_res])

            for g in range(2):
                xt, xn = xt_pool.next()
                hb, hn = hb_pool.next()
                DMA(xt[:], ctok[g * 128:(g + 1) * 128, :], xn, w=[xn])
                A(lambda e, xt=xt, hb=hb: e.activation(out=hb[:], in_=xt[:], func=AF.Silu), r=[xn], w=[hn])
                transpose16(hb, hn, lambda b, g=g: cT[:, g, b * 8:(b + 1) * 8, :], "cT")

            modA = [tsb("modA%d" % g, [128, DM], F32) for g in range(2)]
            modB = [tsb("modB%d" % g, [128, DM], F32) for g in range(2)]
            gb = tsb("gb", [128, DM], F32)
            bada_sb = tsb("bada_sb", [1, 1024], F32)
            wada_pool = Rot([(tsb("wada%d" % i, [128, 1024], BF16), "wada%d" % i) for i in range(5)])

            def mod_piece(piece, dst, dst_names, gs_row=None):
                col0 = piece * DM
                gb_ = gb
                bada_ = bada_sb
                if gs_row is not None:
                    DMA(gb[:], gvecs[gs_row:gs_row + 1, :].to_broadcast([128, DM]), "gb", w=["gb"])
                for half in range(2):
                    banks = [pm.next() for _ in range(4)]
                    DMA(bada_sb[:], b_ada[:, col0 + half * 1024:col0 + (half + 1) * 1024], "bada", w=["bada"])
                    for kc in range(16):
                        wb, wn = wada_pool.next()
                        DMA(wb[:], w_ada[kc * 128:(kc + 1) * 128, col0 + half * 1024: col0 + (half + 1) * 1024], wn,
                            w=[wn], eng="gpsimd")
                        for g in range(2):
                            for nb in range(2):
                                p, pn = banks[g * 2 + nb]
                                T(lambda e, p=p, g=g, nb=nb, kc=kc, wb=wb: e.matmul(
                                    p[:], lhsT=cT[:, g, kc, :], rhs=wb[:, nb * 512:(nb + 1) * 512],
                                    start=(kc == 0), stop=False), r=["cT", wn], w=[pn])
                    for g in range(2):
                        for nb in range(2):
                            p, pn = banks[g * 2 + nb]
                            cc = half * 1024 + nb * 512
                            T(lambda e, p=p, nb=nb: e.matmul(p[:], lhsT=ones_f[0:1, :], rhs=bada_[0:1, nb * 512:(nb + 1) * 512],
                                                             start=False, stop=True), r=["ones_f", "bada"], w=[pn])
                            if gs_row is None:
                                A(lambda e, p=p, g=g, cc=cc: e.activation(out=dst[g][:, cc:cc + 512], in_=p[:], func=AF.Copy),
                                  r=[pn], w=[dst_names[g]])
                            else:
                                V(lambda e, p=p, g=g, cc=cc: e.scalar_tensor_tensor(
                                    out=dst[g][:, cc:cc + 512], in0=p[:], scalar=1.0, in1=gb_[:, cc:cc + 512],
                                    op0=ALU.add, op1=ALU.mult), r=[pn, "gb"], w=[dst_names[g]])

            stg_pool = Rot([(tsb("stg%d" % i, [128, 512], F32), "stg%d" % i) for i in range(2)])

            def mod_stream(pieces, LOOK=8):
                items = [(idx, pc, half, kc) for (idx, pc) in pieces for half in (0, 1) for kc in range(16)]
                loaded = {}
                hold = {}

                gbv = gb[:].bitcast(BF16)
                spool = Rot(list(wada_pool.items) + [(gbv[:, i * 1024:(i + 1) * 1024], "gbw%d" % i) for i in range(4)])

                def issue_load(j):
                    idx, pc, half, kc = items[j]
                    wb, wn = spool.next()
                    DMA(wb[:], w_ada[kc * 128:(kc + 1) * 128, pc * DM + half * 1024: pc * DM + (half + 1) * 1024], wn, w=[wn], eng="gpsimd")
                    loaded[j] = (wb, wn)

                def step(j):
                    idx, pc, half, kc = items[j]
                    if j == 0:
                        G(lambda e: e.memset(stats[:, 63, 3:4], 0.0), w=["gb"] + ["gbw%d" % i for i in range(4)])
                        for jj in range(min(LOOK, len(items))):
                            issue_load(jj)
                    if j + LOOK < len(items):
                        issue_load(j + LOOK)
                    if kc == 0:
                        hold["banks"] = [pm.next() for _ in range(4)]
                        DMA(bada_sb[:], b_ada[:, pc * DM + half * 1024:pc * DM + (half + 1) * 1024], "bada", w=["bada"])
                    banks = hold["banks"]
                    wb, wn = loaded.pop(j)
                    bada_ = bada_sb
                    for g in range(2):
                        for nb in range(2):
                            p, pn = banks[g * 2 + nb]
                            T(lambda e, p=p, g=g, nb=nb, kc=kc, wb=wb: e.matmul(
                                p[:], lhsT=cT[:, g, kc, :], rhs=wb[:, nb * 512:(nb + 1) * 512], start=(kc == 0), stop=False),
                              r=["cT", wn], w=[pn])
                    if kc == 15:
                        for g in range(2):
                            for nb in range(2):
                                p, pn = banks[g * 2 + nb]
                                cc = half * 1024 + nb * 512
                                T(lambda e, p=p, nb=nb: e.matmul(p[:], lhsT=ones_f[0:1, :], rhs=bada_[0:1, nb * 512:(nb + 1) * 512],
                                                                 start=False, stop=True), r=["ones_f", "bada"], w=[pn])
                                sg, sgn = stg_pool.next()
                                A(lambda e, p=p, sg=sg: e.activation(out=sg[:], in_=p[:], func=AF.Copy), r=[pn], w=[sgn])
                                DMA(mods_d[idx, g, :, cc:cc + 512], sg[:], sgn, r=[sgn], w=["mods_d%d" % idx])

                for j in range(len(items)):
                    yield (lambda j=j: step(j))

            def norm_to_T(src_ap, grp, dstT, dst_res, col0, GS, GSn, SH, SHn, src_reads=()):
                xt, xn = xt_pool.next()
                hb, hn = hb_pool.next()
                si = stat_i[0] % 64
                stat_i[0] += 1
                sres = "stat%d" % si
                DMA(xt[:], src_ap, xn, r=list(src_reads), w=[xn])
                A(lambda e: e.activation(out=hb[:], in_=xt[:], func=AF.Square, accum_out=stats[:, si, 0:1]),
                  r=[xn], w=[hn, sres])
                A(lambda e: e.activation(out=stats[:, si, 1:2], in_=stats[:, si, 0:1], func=AF.Sqrt, scale=1.0 / DM, bias=EPS),
                  r=[sres], w=[sres])
                V(lambda e: e.reciprocal(out=stats[:, si, 2:3], in_=stats[:, si, 1:2]), r=[sres], w=[sres])
                V(lambda e: e.scalar_tensor_tensor(out=xt[:], in0=xt[:], scalar=stats[:, si, 2:3], in1=GS[grp][:],
                                                   op0=ALU.mult, op1=ALU.mult), r=[xn, sres, GSn[grp]], w=[xn])
                HS = 1280
                V(lambda e: e.tensor_tensor(out=hb[:, 0:HS], in0=xt[:, 0:HS], in1=SH[grp][:, 0:HS], op=ALU.add), r=[xn, SHn[grp]], w=[hn + "a"])
                G(lambda e: e.tensor_tensor(out=hb[:, HS:DM], in0=xt[:, HS:DM], in1=SH[grp][:, HS:DM], op=ALU.add), r=[xn, SHn[grp]], w=[hn + "b"])
                S.op("vector", lambda e: e.memset(stats[:, si, 3:4], 0.0), [hn + "a", hn + "b"], [hn])
                return lambda: transpose16(hb, hn, lambda b: dstT[:, b * 8:(b + 1) * 8, col0:col0 + 128], dst_res)

            def run_pipelined(parts, extra=None, per=0):
                pend = None
                for a in parts:
                    b = a()
                    if pend is not None:
                        pend()
                    pend = b
                    if extra is not None:
                        for _ in range(per):
                            st = next(extra, None)
                            if st is not None:
                                st()
                if pend is not None:
                    pend()
                if extra is not None:
                    for st in extra:
                        st()

            mAn = ["modA0", "modA1"]
            mBn = ["modB0", "modB1"]
            mod_piece(1, modA, mAn, gs_row=0)
            mod_piece(0, modB, mBn)
            parts = []
            for i in range(NPRE // 128):
                parts.append(lambda i=i: norm_to_T(xpre[i * 128:(i + 1) * 128, :], 0, hTp, "hTp", i * 128, modA, mAn, modB, mBn))
            for i in range(NTILE):
                parts.append(lambda i=i: norm_to_T(xm[i * 128:(i + 1) * 128, :], 0 if i < 8 else 1, hT, "hT", i * 128, modA, mAn, modB, mBn))
            run_pipelined(parts, extra=mod_stream([(0, 2), (1, 4), (2, 3), (3, 5)]), per=6)
            if dbg:
                DMA(dbg_hT, hT[:], "dbg_hT", r=["hT"])
            S.barrier()
            nst.__exit__(None, None, None)

            V(lambda e: e.memset(BDa[:], 0.0), w=["BDa"])
            V(lambda e: e.memset(BDx[:], 0.0), w=["BDx"])
            for c in range(8):
                for hh in range(2):
                    DMA(BDa[hh * 64:(hh + 1) * 64, c, hh * 64:(hh + 1) * 64], lru_wa[2 * c + hh], "BDa",
                        r=[], w=["BDa"], eng="gpsimd")
                    DMA(BDx[hh * 64:(hh + 1) * 64, c, hh * 64:(hh + 1) * 64], lru_wx[2 * c + hh], "BDx",
                        r=[], w=["BDx"], eng="gpsimd")
            lsc = ExitStack()
            lsc.__enter__()
            tsb = lambda name, shape, dt: lsc.enter_context(nc.sbuf_tensor(uq(name), list(shape), dt))
            XB = tsb("XB", [128, 1027], F32)
            XSB_ = tsb("XSs", [128, NSEQ * 11], F32)
            XC = tsb("XC", [128, 1024], F32)
            XCB = tsb("XCB", [128, 1024], BF16)
            RB = tsb("RB", [128, 1024], F32)
            IB = tsb("IB", [128, 1024], F32)
            AB = tsb("AB", [128, 1024], F32)
            A2 = tsb("A2", [128, 1024], F32)
            HB = tsb("HB", [128, 1024], F32)
            GY = tsb("GY", [128, NTOK], F32)
            PRE = tsb("PRE", [128, NTOK], F32)
            SQ = tsb("SQ", [128, NTOK], F32)
            ssq = [tsb("ssq%d" % g, [128, NTOK], F32) for g in range(2)]
            hpre = tsb("hpre", [128, 8], F32)
            xtail = tsb("xtail", [128, 8, 3], F32)
            cxtail = tsb("cxtail", [128, 8, 2], F32)
            hlM = tsb("hlM", [128, 8], F32)
            hlS = tsb("hlS", [128, 8, NSEQ], F32)
            cvM = tsb("cvM", [128, 8, 3], F32)
            cvS = tsb("cvS", [128, 8, NSEQ * 3], F32)
            scM = tsb("scM", [128, 8, 2], F32)
            scS = tsb("scS", [128, 8, NSEQ * 2], F32)
            win_pool = Rot([(tsb("win%d" % i, [128, 16, 128], BF16), "win%d" % i) for i in range(4)])

            def load_w_chunk(wsrc, j):
                wb, wn = win_pool.next()
                DMA(wb[:], wsrc[:, j * 128:(j + 1) * 128].rearrange("(kc p) n -> p kc n", p=128), wn, w=[wn], eng="gpsimd")
                return wb, wn

            def proj(wb, wn, srcT, src_res, c0, n):
                p, pn = pm.next()
                for kc in range(16):
                    T(lambda e, p=p, kc=kc: e.matmul(p[:, 0:n], lhsT=wb[:, kc, :], rhs=srcT[:, kc, c0:c0 + n],
                                                     start=(kc == 0), stop=(kc == 15)), r=[wn, src_res], w=[pn])
                return p, pn

            def lru_core(c, Xv, Xres, nseq, L, h0_ap, h0_res, sample):
                N = nseq * L
                v3 = lambda buf: buf[:, 0:N].rearrange("p (s l) -> p s l", l=L)
                V(lambda e: e.tensor_scalar(out=v3(XC), in0=Xv[:, :, 0:L], scalar1=chv[:, c, 0:1], scalar2=chv[:, c, 4:5],
                                            op0=ALU.mult, op1=ALU.add), r=[Xres, "chv"], w=["XC0"])
                for k in range(1, 4):
                    V(lambda e, k=k: e.scalar_tensor_tensor(out=v3(XC), in0=Xv[:, :, k:k + L], scalar=chv[:, c, k:k + 1],
                                                            in1=v3(XC), op0=ALU.mult, op1=ALU.add),
                      r=[Xres, "chv", "XC0"], w=["XC0"])
                A(lambda e: e.activation(out=XCB[:, 0:N], in_=XC[:, 0:N], func=AF.Copy), r=["XC0"], w=["XCB0"])
                for n0 in range(0, N, 512):
                    n = min(512, N - n0)
                    p, pn = pa.next()
                    T(lambda e, p=p, n0=n0, n=n: e.matmul(p[:, 0:n], lhsT=BDa[:, c, :], rhs=XCB[:, n0:n0 + n], start=True, stop=True),
                      r=["BDa", "XCB0"], w=[pn])
                    A(lambda e, p=p, n0=n0, n=n: e.activation(out=RB[:, n0:n0 + n], in_=p[:, 0:n], func=AF.Sigmoid,
                                                              bias=chv[:, c, 5:6]), r=[pn, "chv"], w=["RB0"])
                    p, pn = pa.next()
                    T(lambda e, p=p, n0=n0, n=n: e.matmul(p[:, 0:n], lhsT=BDx[:, c, :], rhs=XCB[:, n0:n0 + n], start=True, stop=True),
                      r=["BDx", "XCB0"], w=[pn])
                    A(lambda e, p=p, n0=n0, n=n: e.activation(out=IB[:, n0:n0 + n], in_=p[:, 0:n], func=AF.Sigmoid,
                                                              bias=chv[:, c, 6:7]), r=[pn, "chv"], w=["IB0"])
                A(lambda e: e.activation(out=AB[:, 0:N], in_=RB[:, 0:N], func=AF.Exp, scale=chv[:, c, 13:14]), r=["RB0", "chv"], w=["AB0"])
                A(lambda e: e.activation(out=A2[:, 0:N], in_=RB[:, 0:N], func=AF.Exp, scale=chv[:, c, 14:15]), r=["RB0", "chv"], w=["A20"])
                A(lambda e: e.activation(out=A2[:, 0:N], in_=A2[:, 0:N], func=AF.Sqrt, scale=-1.0, bias=1.0), r=["A20"], w=["A20"])
                V(lambda e: e.tensor_tensor(out=IB[:, 0:N], in0=IB[:, 0:N], in1=XC[:, 0:N], op=ALU.mult), r=["IB0", "XC0"], w=["IB0"])
                V(lambda e: e.tensor_tensor(out=IB[:, 0:N], in0=IB[:, 0:N], in1=A2[:, 0:N], op=ALU.mult), r=["IB0", "A20"], w=["IB0"])
                if sample:
                    V(lambda e: e.tensor_tensor(out=v3(A2)[:, :, 0], in0=v3(AB)[:, :, 0], in1=h0_ap, op=ALU.mult),
                      r=["AB0", h0_res], w=["A20"])
                    V(lambda e: e.tensor_tensor(out=v3(IB)[:, :, 0], in0=v3(IB)[:, :, 0], in1=v3(A2)[:, :, 0], op=ALU.add),
                      r=["IB0", "A20"], w=["IB0"])
                    V(lambda e: e.memset(v3(AB)[:, :, 0], 0.0), r=["A20"], w=["AB0"])
                    V(lambda e: e.tensor_tensor_scan(out=HB[:, 0:N], data0=AB[:, 0:N], data1=IB[:, 0:N], initial=0.0,
                                                     op0=ALU.mult, op1=ALU.add), r=["AB0", "IB0"], w=["HB0"])
                else:
                    init = 0.0 if h0_ap is None else h0_ap
                    V(lambda e: e.tensor_tensor_scan(out=HB[:, 0:N], data0=AB[:, 0:N], data1=IB[:, 0:N], initial=init,
                                                     op0=ALU.mult, op1=ALU.add),
                      r=["AB0", "IB0"] + ([h0_res] if h0_ap is not None else []), w=["HB0"])

            SXC = tsb("SXC", [128, 128], F32)
            SXCB = tsb("SXCB", [128, 128], BF16)
            SRB = tsb("SRB", [128, 128], F32)
            SIB = tsb("SIB", [128, 128], F32)
            SAB = tsb("SAB", [128, 128], F32)
            SA2 = tsb("SA2", [128, 128], F32)
            SHB = tsb("SHB", [128, 128], F32)

            def lru_split(c, h0_ap, h0_res, XBt, xbn, smp=None):
                H = 512
                halves = (0, 1)
                rn = lambda base, hf: "%s%d" % (base, hf)
                sl = lambda buf, hf: buf[:, hf * H:(hf + 1) * H]
                s3 = lambda buf: buf[:, 0:128].rearrange("p (s l) -> p s l", l=LS)
                if smp is not None:
                    sXv, sxres, sh0, sh0res = smp
                for hf in halves:
                    xr = [xbn + "0"] if hf == 0 else [xbn + "0", xbn + "1"]
                    V(lambda e, hf=hf: e.tensor_scalar(out=sl(XC, hf), in0=XBt[:, hf * H:hf * H + H], scalar1=chv[:, c, 0:1],
                                                       scalar2=chv[:, c, 4:5], op0=ALU.mult, op1=ALU.add), r=xr + ["chv"], w=[rn("XC", hf)])
                    for k in range(1, 4):
                        V(lambda e, hf=hf, k=k: e.scalar_tensor_tensor(out=sl(XC, hf), in0=XBt[:, hf * H + k:hf * H + k + H],
                                                                       scalar=chv[:, c, k:k + 1], in1=sl(XC, hf), op0=ALU.mult, op1=ALU.add),
                          r=xr + ["chv", rn("XC", hf)], w=[rn("XC", hf)])
                if smp is not None:
                    V(lambda e: e.tensor_scalar(out=s3(SXC), in0=sXv[:, :, 0:LS], scalar1=chv[:, c, 0:1], scalar2=chv[:, c, 4:5],
                                                op0=ALU.mult, op1=ALU.add), r=[sxres, "chv"], w=["SXC"])
                    for k in range(1, 4):
                        V(lambda e, k=k: e.scalar_tensor_tensor(out=s3(SXC), in0=sXv[:, :, k:k + LS], scalar=chv[:, c, k:k + 1],
                                                                in1=s3(SXC), op0=ALU.mult, op1=ALU.add), r=[sxres, "chv", "SXC"], w=["SXC"])
                for hf in halves:
                    A(lambda e, hf=hf: e.activation(out=sl(XCB, hf), in_=sl(XC, hf), func=AF.Copy), r=[rn("XC", hf)], w=[rn("XCB", hf)])
                if smp is not None:
                    A(lambda e: e.activation(out=SXCB[:], in_=SXC[:], func=AF.Copy), r=["SXC"], w=["SXCB"])
                jobs = [(sl(XCB, hf), rn("XCB", hf), sl(RB, hf), rn("RB", hf), sl(IB, hf), rn("IB", hf), H) for hf in halves]
                if smp is not None:
                    jobs.append((SXCB[:], "SXCB", SRB[:], "SRB", SIB[:], "SIB", 128))
                for (xin, xinr, rout, routr, iout, ioutr, n) in jobs:
                    p, pn = pa.next()
                    T(lambda e, p=p, xin=xin, n=n: e.matmul(p[:, 0:n], lhsT=BDa[:, c, :], rhs=xin, start=True, stop=True),
                      r=["BDa", xinr], w=[pn])
                    A(lambda e, p=p, rout=rout, n=n: e.activation(out=rout, in_=p[:, 0:n], func=AF.Sigmoid, bias=chv[:, c, 5:6]),
                      r=[pn, "chv"], w=[routr])
                    p, pn = pa.next()
                    T(lambda e, p=p, xin=xin, n=n: e.matmul(p[:, 0:n], lhsT=BDx[:, c, :], rhs=xin, start=True, stop=True),
                      r=["BDx", xinr], w=[pn])
                    A(lambda e, p=p, iout=iout, n=n: e.activation(out=iout, in_=p[:, 0:n], func=AF.Sigmoid, bias=chv[:, c, 6:7]),
                      r=[pn, "chv"], w=[ioutr])
                ej = [(sl(RB, hf), rn("RB", hf), sl(AB, hf), rn("AB", hf), sl(A2, hf), rn("A2", hf)) for hf in halves]
                if smp is not None:
                    ej.append((SRB[:], "SRB", SAB[:], "SAB", SA2[:], "SA2"))
                for (rin, rinr, aout, aoutr, a2out, a2r) in ej:
                    A(lambda e, rin=rin, aout=aout: e.activation(out=aout, in_=rin, func=AF.Exp, scale=chv[:, c, 13:14]),
                      r=[rinr, "chv"], w=[aoutr])
                    A(lambda e, rin=rin, a2out=a2out: e.activation(out=a2out, in_=rin, func=AF.Exp, scale=chv[:, c, 14:15]),
                      r=[rinr, "chv"], w=[a2r])
                for (rin, rinr, aout, aoutr, a2out, a2r) in ej:
                    A(lambda e, a2out=a2out: e.activation(out=a2out, in_=a2out, func=AF.Sqrt, scale=-1.0, bias=1.0), r=[a2r], w=[a2r])
                uj = [(sl(IB, hf), rn("IB", hf), sl(XC, hf), rn("XC", hf), sl(A2, hf), rn("A2", hf)) for hf in halves]
                if smp is not None:
                    uj.append((SIB[:], "SIB", SXC[:], "SXC", SA2[:], "SA2"))
                for (ib, ibr, xc_, xcr, a2_, a2r) in uj:
                    V(lambda e, ib=ib, xc_=xc_: e.tensor_tensor(out=ib, in0=ib, in1=xc_, op=ALU.mult), r=[ibr, xcr], w=[ibr])
                    V(lambda e, ib=ib, a2_=a2_: e.tensor_tensor(out=ib, in0=ib, in1=a2_, op=ALU.mult), r=[ibr, a2r], w=[ibr])
                for hf in halves:
                    if hf == 0:
                        init = 0.0 if h0_ap is None else h0_ap
                        rr = [h0_res] if h0_ap is not None else []
                    else:
                        init = HB[:, H - 1:H]
                        rr = ["HB0"]
                    V(lambda e, hf=hf, init=init: e.tensor_tensor_scan(out=sl(HB, hf), data0=sl(AB, hf), data1=sl(IB, hf), initial=init,
                                                                       op0=ALU.mult, op1=ALU.add),
                      r=[rn("AB", hf), rn("IB", hf)] + rr, w=[rn("HB", hf)])
                if smp is not None:
                    V(lambda e: e.tensor_tensor(out=s3(SA2)[:, :, 0], in0=s3(SAB)[:, :, 0], in1=sh0, op=ALU.mult),
                      r=["SAB", "SA2", sh0res], w=["SA2"])
                    V(lambda e: e.tensor_tensor(out=s3(SIB)[:, :, 0], in0=s3(SIB)[:, :, 0], in1=s3(SA2)[:, :, 0], op=ALU.add),
                      r=["SIB", "SA2"], w=["SIB"])
                    V(lambda e: e.memset(s3(SAB)[:, :, 0], 0.0), r=["SA2"], w=["SAB"])
                    V(lambda e: e.tensor_tensor_scan(out=SHB[:], data0=SAB[:], data1=SIB[:], initial=0.0, op0=ALU.mult, op1=ALU.add),
                      r=["SAB", "SIB"], w=["SHB"])

            XBv = XB[:, :].rearrange("p (s l) -> p s l", s=1)
            XSv = XSB_[:, :].rearrange("p (s l) -> p s l", l=11)
            XB2 = tsb("XB2", [128, 1027], F32)
            XS2 = tsb("XS2", [128, NSEQ * 11], F32)
            GY2 = tsb("GY2", [128, NTOK], F32)
            bsets = [dict(XB=XB, xbn="XB", XS=XSB_, xsn="XSs", GY=GY, gyn="GY"),
                     dict(XB=XB2, xbn="XBb", XS=XS2, xsn="XSb", GY=GY2, gyn="GYb")]

            if True:
                def P_pre(c, st):
                    XBt, xbn = st["XB"], st["xbn"]
                    wb, wn = load_w_chunk(w_in, c)
                    V(lambda e: e.memset(XBt[:, 0:3], 0.0), w=[xbn + "0"])
                    for nb in range(2):
                        p, pn = proj(wb, wn, hTp, "hTp", nb * 512, 512)
                        A(lambda e, p=p, nb=nb: e.activation(out=XBt[:, 3 + nb * 512:3 + (nb + 1) * 512], in_=p[:], func=AF.Copy),
                          r=[pn], w=[xbn + "%d" % nb])

                def E_pre(c, st):
                    XBt, xbn = st["XB"], st["xbn"]
                    lru_split(c, None, None, XBt, xbn)
                    V(lambda e: e.tensor_scalar(out=hpre[:, c:c + 1], in0=HB[:, 1023:1024], scalar1=flag[:, 0:1], scalar2=None,
                                                op0=ALU.mult), r=["HB1", "flag"], w=["hpre"])
                    V(lambda e: e.tensor_scalar(out=xtail[:, c, :], in0=XBt[:, 1024:1027], scalar1=flag[:, 0:1], scalar2=None,
                                                op0=ALU.mult), r=[xbn + "1", "flag"], w=["xtail"])

                P_pre(0, bsets[0])
                for c in range(8):
                    if c + 1 < 8:
                        P_pre(c + 1, bsets[(c + 1) % 2])
                    E_pre(c, bsets[c % 2])
                for c in range(8):
                    wbx, wnx = load_w_chunk(w_in, 32 + c)
                    wbc, wnc = load_w_chunk(w_in, 24 + c)
                    px, pxn = proj(wbx, wnx, hTp, "hTp", NPRE - 8, 8)
                    A(lambda e, px=px: e.activation(out=SQ[:, 0:8], in_=px[:, 0:8], func=AF.Copy), r=[pxn], w=["SQ"])
                    pc, pcn = proj(wbc, wnc, hTp, "hTp", NPRE - 8, 8)
                    V(lambda e, pc=pc: e.tensor_tensor(out=SQ[:, 0:8], in0=pc[:, 0:8], in1=SQ[:, 0:8], op=ALU.mult),
                      r=[pcn, "SQ"], w=["SQ"])
                    V(lambda e, c=c: e.tensor_scalar(out=cxtail[:, c, :], in0=SQ[:, 6:8], scalar1=flag[:, 0:1], scalar2=None,
                                                     op0=ALU.mult), r=["SQ", "flag"], w=["cxtail"])
                S.barrier()


            def finish_chunk(cidx, g, first):
                DMA(pre_d[cidx], PRE[:], "pre_d%d" % cidx, r=["PRE"], w=["pre_d%d" % cidx])
                A(lambda e: e.activation(out=SQ[:], in_=PRE[:], func=AF.Square), r=["PRE"], w=["SQ"])
                for (n0, n) in ((0, 512), (512, 512), (1024, 128)):
                    p, pn = pa.next()
                    T(lambda e, p=p, n0=n0, n=n: e.matmul(p[:, 0:n], lhsT=ones_f[:], rhs=SQ[:, n0:n0 + n], start=True, stop=True),
                      r=["ones_f", "SQ"], w=[pn])
                    if first:
                        V(lambda e, p=p, n0=n0, n=n: e.tensor_copy(out=ssq[g][:, n0:n0 + n], in_=p[:, 0:n]), r=[pn], w=["ssq%d" % g])
                    else:
                        V(lambda e, p=p, n0=n0, n=n: e.tensor_tensor(out=ssq[g][:, n0:n0 + n], in0=p[:, 0:n], in1=ssq[g][:, n0:n0 + n],
                                                                     op=ALU.add), r=[pn, "ssq%d" % g], w=["ssq%d" % g])

            blocks = ((0, 512), (512, 512), (1024, 128))

            def P_main(c, st):
                XBt, xbn, XSt, xsn, GYt, gyn = st["XB"], st["xbn"], st["XS"], st["xsn"], st["GY"], st["gyn"]
                XSv_ = XSt[:, :].rearrange("p (s l) -> p s l", l=11)
                wb, wn = load_w_chunk(w_in, c)
                V(lambda e: e.tensor_copy(out=XBt[:, 0:3], in_=xtail[:, c, :]), r=["xtail"], w=[xbn + "0"])
                V(lambda e: e.tensor_copy(out=XSv_[:, :, 0:3], in_=stT[:, c, 0:48].rearrange("p (s k) -> p s k", k=3)),
                  r=["stT"], w=[xsn])
                for (n0, n) in blocks:
                    p, pn = proj(wb, wn, hT, "hT", n0, n)
                    if n0 < 1024:
                        A(lambda e, p=p, n0=n0: e.activation(out=XBt[:, 3 + n0:3 + n0 + 512], in_=p[:], func=AF.Copy), r=[pn],
                          w=[xbn + "%d" % (n0 // 512)])
                    else:
                        A(lambda e, p=p: e.activation(out=XSv_[:, :, 3:11], in_=p[:, 0:128].rearrange("p (s l) -> p s l", l=8),
                                                      func=AF.Copy), r=[pn], w=[xsn])
                wb2, wn2 = load_w_chunk(w_in, 8 + c)
                for (n0, n) in blocks:
                    p, pn = proj(wb2, wn2, hT, "hT", n0, n)
                    A(lambda e, p=p, n0=n0, n=n: e.activation(out=GYt[:, n0:n0 + n], in_=p[:, 0:n], func=AF.Gelu_apprx_tanh),
                      r=[pn], w=[gyn])

            def E_main(c, st):
                XBt, xbn, XSt, xsn, GYt, gyn = st["XB"], st["xbn"], st["XS"], st["xsn"], st["GY"], st["gyn"]
                XSv_ = XSt[:, :].rearrange("p (s l) -> p s l", l=11)
                lru_split(c, hpre[:, c:c + 1], "hpre", XBt, xbn, smp=(XSv_, xsn, stT[:, c, 80:96], "stT"))
                V(lambda e: e.tensor_tensor(out=PRE[:, 0:1024], in0=HB[:, 0:1024], in1=GYt[:, 0:1024], op=ALU.mult),
                  r=["HB0", "HB1", gyn], w=["PRE"])
                V(lambda e: e.tensor_copy(out=hlM[:, c:c + 1], in_=HB[:, 1023:1024]), r=["HB1"], w=["hlM"])
                V(lambda e: e.tensor_copy(out=cvM[:, c, :], in_=XBt[:, 1024:1027]), r=[xbn + "1"], w=["cvM"])
                V(lambda e: e.tensor_tensor(out=PRE[:, 1024:NTOK], in0=SHB[:], in1=GYt[:, 1024:NTOK], op=ALU.mult),
                  r=["SHB", gyn], w=["PRE"])
                V(lambda e: e.tensor_copy(out=hlS[:, c, :], in_=SHB[:].rearrange("p (s l) -> p s l", l=8)[:, :, 7]),
                  r=["SHB"], w=["hlS"])
                V(lambda e: e.tensor_copy(out=cvS[:, c, :].rearrange("p (s k) -> p s k", k=3), in_=XSv_[:, :, 8:11]),
                  r=[xsn], w=["cvS"])
                finish_chunk(c, 0, c == 0)

            P_main(0, bsets[0])
            for c in range(8):
                if c + 1 < 8:
                    P_main(c + 1, bsets[(c + 1) % 2])
                E_main(c, bsets[c % 2])

            S.barrier()
            CXv = XB[:, 0:1026].rearrange("p (s l) -> p s l", s=1)
            CSv = XSB_[:, 0:NSEQ * 10].rearrange("p (s l) -> p s l", l=10)
            for c in range(8):
                wbx, wnx = load_w_chunk(w_in, 32 + c)
                wbc, wnc = load_w_chunk(w_in, 24 + c)
                wbb, wnb = load_w_chunk(w_in, 16 + c)
                V(lambda e, c=c: e.tensor_copy(out=XB[:, 0:2], in_=cxtail[:, c, :]), r=["cxtail"], w=["XB"])
                V(lambda e, c=c: e.tensor_copy(out=CSv[:, :, 0:2], in_=stT[:, c, 48:80].rearrange("p (s k) -> p s k", k=2)),
                  r=["stT"], w=["XSs"])
                for (n0, n) in blocks:
                    px, pxn = proj(wbx, wnx, hT, "hT", n0, n)
                    A(lambda e, px=px, n0=n0, n=n: e.activation(out=GY[:, n0:n0 + n], in_=px[:, 0:n], func=AF.Copy), r=[pxn], w=["GY"])
                    pc, pcn = proj(wbc, wnc, hT, "hT", n0, n)
                    if n0 < 1024:
                        V(lambda e, pc=pc, n0=n0: e.tensor_tensor(out=XB[:, 2 + n0:2 + n0 + 512], in0=pc[:], in1=GY[:, n0:n0 + 512],
                                                                  op=ALU.mult), r=[pcn, "GY"], w=["XB"])
                    else:
                        V(lambda e, pc=pc: e.tensor_tensor(out=CSv[:, :, 2:10], in0=pc[:, 0:128].rearrange("p (s l) -> p s l", l=8),
                                                           in1=GY[:, 1024:NTOK].rearrange("p (s l) -> p s l", l=8), op=ALU.mult),
                          r=[pcn, "GY"], w=["XSs"])
                for (Xv_, res, nseq, L, dst, dres) in ((CXv, "XB", 1, 1024, XC, "XC"), (CSv, "XSs", NSEQ, LS, A2, "A2")):
                    N = nseq * L
                    d3 = dst[:, 0:N].rearrange("p (s l) -> p s l", l=L)
                    V(lambda e, Xv_=Xv_, d3=d3, L=L, c=c: e.tensor_scalar(out=d3, in0=Xv_[:, :, 0:L], scalar1=chv[:, c, 8:9], scalar2=None,
                                                                     op0=ALU.mult), r=[res, "chv"], w=[dres])
                    for k in range(1, 3):
                        V(lambda e, Xv_=Xv_, d3=d3, L=L, k=k, c=c: e.scalar_tensor_tensor(
                            out=d3, in0=Xv_[:, :, k:k + L], scalar=chv[:, c, 8 + k:9 + k], in1=d3, op0=ALU.mult, op1=ALU.add),
                          r=[res, "chv", dres], w=[dres])
                for (n0, n) in blocks:
                    pbk, pbn = proj(wbb, wnb, hT, "hT", n0, n)
                    if n0 < 1024:
                        V(lambda e, pbk=pbk, n0=n0: e.tensor_tensor(out=PRE[:, n0:n0 + 512], in0=pbk[:], in1=XC[:, n0:n0 + 512],
                                                                    op=ALU.mult), r=[pbn, "XC"], w=["PRE"])
                    else:
                        V(lambda e, pbk=pbk: e.tensor_tensor(out=PRE[:, 1024:NTOK], in0=pbk[:, 0:128], in1=A2[:, 0:128], op=ALU.mult),
                          r=[pbn, "A2"], w=["PRE"])
                V(lambda e, c=c: e.tensor_copy(out=scM[:, c, :], in_=XB[:, 1024:1026]), r=["XB"], w=["scM"])
                V(lambda e, c=c: e.tensor_copy(out=scS[:, c, :].rearrange("p (s k) -> p s k", k=2), in_=CSv[:, :, 8:10]),
                  r=["XSs"], w=["scS"])
                finish_chunk(8 + c, 1, c == 0)

            def out_T(src_ap, src_res, nrows, dst_fn):
                for c in range(8):
                    p, pn = pa.next()
                    T(lambda e, p=p, c=c: e.transpose(out=p[0:nrows, 0:128], in_=src_ap(c), identity=ident_f[:]),
                      r=[src_res, "ident_f"], w=[pn])
                    V(lambda e, p=p, c=c: e.tensor_copy(out=SQ[0:nrows, c * 128:(c + 1) * 128], in_=p[0:nrows, 0:128]), r=[pn], w=["SQ"])
                dst_fn()

            out_T(lambda c: cvS[:, c, :], "cvS", 48, lambda: DMA(cvs_d, SQ[0:48, 0:1024], "o_cvs", r=["SQ"]))
            out_T(lambda c: scS[:, c, :], "scS", 32, lambda: DMA(scs_d, SQ[0:32, 0:1024], "o_scs", r=["SQ"]))
            out_T(lambda c: hlS[:, c, :], "hlS", 16, lambda: DMA(hls_d, SQ[0:16, 0:1024], "o_hls", r=["SQ"]))
            out_T(lambda c: cvM[:, c, :], "cvM", 3, lambda: DMA(cvp_d, SQ[0:3, 0:1024], "o_cvp", r=["SQ"]))
            out_T(lambda c: scM[:, c, :], "scM", 2, lambda: DMA(scp_d, SQ[0:2, 0:1024], "o_scp", r=["SQ"]))
            p, pn = pa.next()
            T(lambda e, p=p: e.transpose(out=p[0:8, 0:128], in_=hlM[:, 0:8], identity=ident_f[:]), r=["hlM", "ident_f"], w=[pn])
            V(lambda e, p=p: e.tensor_copy(out=SQ[0:8, 0:128], in_=p[0:8, 0:128]), r=[pn], w=["SQ"])
            DMA(hlp_d, SQ[0:8, 0:128], "o_hlp", r=["SQ"])

            S.barrier()
            mixT = hT
            for g in range(2):
                V(lambda e, g=g: e.tensor_scalar(out=ssq[g][:], in0=ssq[g][:], scalar1=1.0 / 1024, scalar2=EPS, op0=ALU.mult, op1=ALU.add),
                  r=["ssq%d" % g], w=["ssq%d" % g])
                A(lambda e, g=g: e.activation(out=ssq[g][:], in_=ssq[g][:], func=AF.Sqrt), r=["ssq%d" % g], w=["ssq%d" % g])
                V(lambda e, g=g: e.reciprocal(out=ssq[g][:], in_=ssq[g][:]), r=["ssq%d" % g], w=["ssq%d" % g])
            pre_pool = Rot([(PRE, "PRE"), (SQ, "SQ"), (GY, "GY")])
            for cidx in range(16):
                g = cidx // 8
                c = cidx % 8
                bufp, bn = pre_pool.next()
                DMA(bufp[:], pre_d[cidx], bn, r=["pre_d%d" % cidx], w=[bn])
                V(lambda e, bufp=bufp, cidx=cidx, g=g, c=c: e.scalar_tensor_tensor(
                    out=mixT[:, cidx, :], in0=bufp[:], scalar=chv[:, c, 11 + g:12 + g], in1=ssq[g][:], op0=ALU.mult, op1=ALU.mult),
                  r=[bn, "chv", "ssq%d" % g], w=["mixT"])
            if dbg:
                DMA(dbg_mix, mixT[:], "dbg_mix", r=["mixT"])

            S.barrier()
            lsc.__exit__(None, None, None)
            wsc = ExitStack()
            wsc.__enter__()
            tsb = lambda name, shape, dt: wsc.enter_context(nc.sbuf_tensor(uq(name), list(shape), dt))
            modA = [tsb("modA%d" % g, [128, DM], F32) for g in range(2)]
            gb = tsb("gb", [128, DM], F32)
            bada_sb = tsb("bada_sb", [1, 1024], F32)
            wada_pool = Rot([(tsb("wada%d" % i, [128, 1024], BF16), "wada%d" % i) for i in range(5)])
            for g in range(2):
                DMA(modA[g][:], mods_d[0, g], mAn[g], r=["mods_d0"], w=[mAn[g]])
            wo_pool = Rot([(tsb("wo%d" % i, [128, 16, 512], BF16), "wo%d" % i) for i in range(2)])
            xr_pool = Rot([(tsb("xr%d" % i, [128, 512], F32), "xr%d" % i) for i in range(3)])
            wtmp = tsb("wtmp", [128, 512], F32)
            for db in range(4):
                wo, won = wo_pool.next()
                DMA(wo[:], w_out[:, db * 512:(db + 1) * 512].rearrange("(kc p) n -> p kc n", p=128), won, w=[won], eng="gpsimd")
                for i in range(NTILE):
                    g = 0 if i < 8 else 1
                    p, pn = pm.next()
                    for kc in range(16):
                        T(lambda e, p=p, kc=kc, i=i, wo=wo: e.matmul(p[:], lhsT=mixT[:, kc, i * 128:(i + 1) * 128], rhs=wo[:, kc, :],
                                                                     start=(kc == 0), stop=(kc == 15)), r=["mixT", won], w=[pn])
                    xr, xrn = xr_pool.next()
                    DMA(xr[:], xm[i * 128:(i + 1) * 128, db * 512:(db + 1) * 512], xrn, w=[xrn])
                    V(lambda e, p=p, g=g, db=db, xr=xr, mA=modA: e.tensor_tensor(out=wtmp[:], in0=p[:],
                                                                                 in1=mA[g][:, db * 512:(db + 1) * 512], op=ALU.mult),
                      r=[pn, mAn[g]], w=["wtmp"])
                    V(lambda e, xr=xr: e.tensor_tensor(out=xr[:], in0=xr[:], in1=wtmp[:], op=ALU.add),
                      r=[xrn, "wtmp"], w=[xrn])
                    DMA(x1_d[i * 128:(i + 1) * 128, db * 512:(db + 1) * 512], xr[:], xrn, r=[xrn], w=["x1_d%d" % i])

            S.barrier()
            wsc.__exit__(None, None, None)
            n2s = ExitStack()
            n2s.__enter__()
            tsb = lambda name, shape, dt: n2s.enter_context(nc.sbuf_tensor(uq(name), list(shape), dt))
            xt_pool = Rot([(tsb("xt0", [128, DM], F32), "xt0"), (tsb("xt1", [128, DM], F32), "xt1")])
            hb_pool = Rot([(tsb("hb0", [128, DM], BF16), "hb0"), (tsb("hb1", [128, DM], BF16), "hb1")])
            modA = [tsb("modA%d" % g, [128, DM], F32) for g in range(2)]
            modB = [tsb("modB%d" % g, [128, DM], F32) for g in range(2)]
            gb = tsb("gb", [128, DM], F32)
            bada_sb = tsb("bada_sb", [1, 1024], F32)
            wada_pool = Rot([(tsb("wada%d" % i, [128, 1024], BF16), "wada%d" % i) for i in range(5)])
            DMA(gb[:], gvecs[1:2, :].to_broadcast([128, DM]), "gb", w=["gb"])
            for g in range(2):
                DMA(modA[g][:], mods_d[1, g], mAn[g], r=["mods_d1"], w=[mAn[g]])
                DMA(modB[g][:], mods_d[2, g], mBn[g], r=["mods_d2"], w=[mBn[g]])
                V(lambda e, g=g, mA=modA, gb_=gb: e.scalar_tensor_tensor(out=mA[g][:], in0=mA[g][:], scalar=1.0, in1=gb_[:],
                                                                        op0=ALU.add, op1=ALU.mult), r=[mAn[g], "gb"], w=[mAn[g]])
            run_pipelined([lambda i=i: norm_to_T(x1_d[i * 128:(i + 1) * 128, :], 0 if i < 8 else 1, h2T, "h2T", i * 128,
                                                 modA, mAn, modB, mBn, src_reads=["x1_d%d" % i]) for i in range(NTILE)])
            if dbg:
                DMA(dbg_h2T, h2T[:], "dbg_h2T", r=["h2T"])
            S.barrier()
            n2s.__exit__(None, None, None)

        if upto >= 2:
            est = ExitStack()
            with est:
                def esb(name, shape, dt):
                    return est.enter_context(nc.sbuf_tensor(uq(name), list(shape), dt))

                qT = esb("qT", [128, 16, NTOK], BF16)
                PT = esb("PT", [128, 16, NTOK], F32)
                thb = esb("thb", [128, 2, 72], F32)
                wqs = ExitStack()
                wqs.__enter__()
                win_pool = Rot([(wqs.enter_context(nc.sbuf_tensor(uq("wq%d" % i), [128, 16, 128], BF16)), "wq%d" % i) for i in range(2)])
                for hp in range(16):
                    wb, wn = win_pool.next()
                    DMA(wb[:], wq[:, hp * 128:(hp + 1) * 128].rearrange("(kc p) n -> p kc n", p=128), wn, w=[wn], eng="gpsimd")
                    for tt in range(3):
                        p, pn = pm.next()
                        for kc in range(16):
                            T(lambda e, p=p, kc=kc, tt=tt, wb=wb: e.matmul(p[:, 0:TT], lhsT=wb[:, kc, :], rhs=h2T[:, kc, tt * TT:(tt + 1) * TT],
                                                                           start=(kc == 0), stop=(kc == 15)), r=[wn, "h2T"], w=[pn])
                        A(lambda e, p=p, hp=hp, tt=tt: e.activation(out=qT[:, hp, tt * TT:(tt + 1) * TT], in_=p[:, 0:TT], func=AF.Copy),
                          r=[pn], w=["qT"])
                if dbg:
                    DMA(dbg_qT, qT[:], "dbg_qT", r=["qT"])
                S.barrier()
                wqs.__exit__(None, None, None)
                tst2 = ExitStack()
                with tst2:
                    def t2(name, shape, dt):
                        return tst2.enter_context(nc.sbuf_tensor(uq(name), list(shape), dt))
                    vtop = t2("vtop", [128, 16, 24], F32)
                    scs = t2("scs", [128, 16, 128], F32)
                    wk4 = t2("wk4", [128, 4, 128], F32)
                    cand = t2("cand", [128, 8, 289], F32)
                    wk2 = t2("wk2", [128, 4, 289], F32)
                    ctop = t2("ctop", [128, 8, 24], F32)
                    dlt = t2("dlt", [128, 8, 16], F32)
                    zz = t2("zz", [128, 2, 8], F32)
                    AX = mybir.AxisListType
                    for i in range(NTILE):
                        banks = [pm.next() for _ in range(4)]
                        for hp in range(16):
                            p, pn = banks[hp // 4]
                            o = (hp % 4) * 128
                            T(lambda e, p=p, o=o, hp=hp, i=i: e.matmul(p[:, o:o + 128], lhsT=qT[:, hp, i * 128:(i + 1) * 128],
                                                                       rhs=KT[:, hp % 2, :], start=True, stop=True),
                              r=["qT", "KT"], w=[pn])
                        for b_ in range(4):
                            p, pn = banks[b_]
                            A(lambda e, p=p, b_=b_: e.activation(out=scs[:, b_ * 4:(b_ + 1) * 4, :],
                                                                 in_=p[:].rearrange("p (q n) -> p q n", n=128), func=AF.Copy),
                              r=[pn], w=["scs%d" % b_])
                        for b_ in range(4):
                            hps = [b_ * 4 + j for j in range(4)]
                            sr = "scs%d" % b_
                            for j, hp in enumerate(hps):
                                V(lambda e, hp=hp: e.max(out=vtop[:, hp, 0:8], in_=scs[:, hp, :]), r=[sr], w=["vtop%d" % hp])
                            for j, hp in enumerate(hps):
                                V(lambda e, hp=hp, j=j: e.match_replace(out=wk4[:, j, :], in_to_replace=vtop[:, hp, 0:8], in_values=scs[:, hp, :],
                                                                        imm_value=-1e30), r=[sr, "vtop%d" % hp], w=["wk4_%d" % j])
                            for j, hp in enumerate(hps):
                                V(lambda e, hp=hp, j=j: e.max(out=vtop[:, hp, 8:16], in_=wk4[:, j, :]), r=["wk4_%d" % j], w=["vtop%d" % hp])
                            for j, hp in enumerate(hps):
                                V(lambda e, hp=hp, j=j: e.match_replace(out=wk4[:, j, :], in_to_replace=vtop[:, hp, 8:16], in_values=wk4[:, j, :],
                                                                        imm_value=-1e30), r=["wk4_%d" % j, "vtop%d" % hp], w=["wk4_%d" % j])
                            for j, hp in enumerate(hps):
                                V(lambda e, hp=hp, j=j: e.max(out=vtop[:, hp, 16:24], in_=wk4[:, j, :]), r=["wk4_%d" % j], w=["vtop%d" % hp])
                        vt4 = vtop[:].rearrange("q (h p) k -> q h p k", p=2)
                        V(lambda e, vt4=vt4: e.tensor_tensor(
                            out=cand[:].rearrange("q h (a b) -> q h a b", b=17),
                            in0=vt4[:, :, 0, 0:17].unsqueeze(3).to_broadcast([128, 8, 17, 17]),
                            in1=vt4[:, :, 1, 0:17].unsqueeze(2).to_broadcast([128, 8, 17, 17]), op=ALU.add),
                          r=["vtop%d" % hp for hp in range(16)], w=["cand"])
                        for b_ in range(2):
                            hs = [b_ * 4 + j for j in range(4)]
                            for j, h in enumerate(hs):
                                V(lambda e, h=h: e.max(out=ctop[:, h, 0:8], in_=cand[:, h, :]), r=["cand"], w=["ctop%d" % h])
                            for j, h in enumerate(hs):
                                V(lambda e, h=h, j=j: e.match_replace(out=wk2[:, j, :], in_to_replace=ctop[:, h, 0:8], in_values=cand[:, h, :],
                                                                      imm_value=-1e30), r=["cand", "ctop%d" % h], w=["wk2_%d" % j])
                            for j, h in enumerate(hs):
                                V(lambda e, h=h, j=j: e.max(out=ctop[:, h, 8:16], in_=wk2[:, j, :]), r=["wk2_%d" % j], w=["ctop%d" % h])
                            for j, h in enumerate(hs):
                                V(lambda e, h=h, j=j: e.match_replace(out=wk2[:, j, :], in_to_replace=ctop[:, h, 8:16], in_values=wk2[:, j, :],
                                                                      imm_value=-1e30), r=["wk2_%d" % j, "ctop%d" % h], w=["wk2_%d" % j])
                            for j, h in enumerate(hs):
                                V(lambda e, h=h, j=j: e.max(out=ctop[:, h, 16:24], in_=wk2[:, j, :]), r=["wk2_%d" % j], w=["ctop%d" % h])
                        call = ["ctop%d" % h for h in range(8)]
                        i8 = slice(i * 8, (i + 1) * 8)
                        V(lambda e, i8=i8: e.tensor_tensor(out=thb[:, 0, i8], in0=ctop[:, :, 15], in1=ctop[:, :, 16], op=ALU.add),
                          r=call, w=["thb"])
                        V(lambda e, i8=i8: e.tensor_scalar(out=thb[:, 0, i8], in0=thb[:, 0, i8], scalar1=0.5, scalar2=None, op0=ALU.mult),
                          r=["thb"], w=["thb"])
                        V(lambda e: e.tensor_tensor(out=dlt[:], in0=ctop[:, :, 0:16], in1=ctop[:, :, 0:1].to_broadcast([128, 8, 16]),
                                                    op=ALU.subtract), r=call, w=["dlt"])
                        A(lambda e: e.activation(out=dlt[:], in_=dlt[:], func=AF.Exp), r=["dlt"], w=["dlt"])
                        V(lambda e: e.tensor_reduce(out=zz[:, 0, :], in_=dlt[:], axis=AX.X, op=ALU.add), r=["dlt"], w=["zz"])
                        A(lambda e: e.activation(out=zz[:, 1, :], in_=zz[:, 0, :], func=AF.Ln), r=["zz"], w=["zz"])
                        V(lambda e, i8=i8: e.scalar_tensor_tensor(out=thb[:, 1, i8], in0=ctop[:, :, 0], scalar=-1.0, in1=zz[:, 1, :],
                                                                  op0=ALU.mult, op1=ALU.subtract), r=call + ["zz"], w=["thb"])
                    if dbg:
                        DMA(dbg_th, thb[:], "dbg_th", r=["thb"])
                    S.barrier()

                if upto >= 3:
                    psD = Rot([(pf[0], "pf0"), (pf[1], "pf1"), (pf[4], "pf4"), (pf[5], "pf5")])
                    psG = Rot([(pf[2], "pf2"), (pf[3], "pf3")])
                    psS = Rot([(pf[3], "pf3"), (pf[4], "pf4"), (pf[5], "pf5")])
                    psO = Rot([(pf[4], "pf4"), (pf[5], "pf5")])
                    lst = ExitStack()
                    lst.__enter__()
                    esb_outer = esb
                    esb = lambda name, shape, dt: lst.enter_context(nc.sbuf_tensor(uq(name), list(shape), dt))
                    UTp = Rot([(esb("UT%d" % i, [128, 16, 128], BF16), "UT%d" % i) for i in range(4)])
                    Vb = esb("Vb", [128, GCH, DM], BF16)
                    AT = esb("AT", [128, GCH, NTOK], BF16)
                    Ebuf = Rot([(esb("Eb%d" % i, [128, 512], BF16), "Eb%d" % i) for i in range(4)])
                    Gbuf = Rot([(esb("Gb%d" % i, [128, 512], BF16), "Gb%d" % i) for i in range(4)])
                    Kx1 = Rot([(esb("Kx1_%d" % i, [128, GCH, 128], BF16), "Kx1_%d" % i) for i in range(2)])
                    Kx2 = esb("Kx2", [128, GCH, 128], BF16)
                    V(lambda e: e.tensor_copy(out=Kx2[:], in_=KT[:, 1, :].unsqueeze(1).to_broadcast([128, GCH, 128])), r=["KT"], w=["Kx2"])
                    utslots = {}

                    def load_u(n):
                        ut_, utn_ = UTp.next()
                        DMA(ut_[:], puT[:, n * 128:(n + 1) * 128].rearrange("(dc p) e -> p dc e", p=128), utn_, w=[utn_], eng="gpsimd")
                        utslots[n] = (ut_, utn_)

                    def do_T(n):
                        pass

                    load_u(0)
                    load_u(1)
                    load_u(2)
                    for grp in range(NGRP):
                        k1, k1n = Kx1.next()
                        V(lambda e, k1=k1, grp=grp: e.tensor_copy(
                            out=k1[:], in_=KT[:, 0, grp * GCH:(grp + 1) * GCH].unsqueeze(2).to_broadcast([128, GCH, 128])),
                          r=["KT"], w=[k1n])
                        for cc in range(GCH):
                            ech = grp * GCH + cc
                            DMA(Vb[:, cc, :], pv[ech * 128:(ech + 1) * 128, :], "Vb%d" % cc, w=["Vb%d" % cc], eng="gpsimd")
                        for cc in range(GCH):
                            n = grp * GCH + cc
                            if n + 3 < 128:
                                load_u(n + 3)
                            ut, utn = utslots.pop(n)
                            for tt in range(3):
                                p, pn = psS.next()
                                for dc in range(16):
                                    T(lambda e, p=p, dc=dc, tt=tt, ut=ut: e.matmul(p[:, 0:TT], lhsT=ut[:, dc, :], rhs=h2T[:, dc, tt * TT:(tt + 1) * TT],
                                                                                   start=(dc == 0), stop=(dc == 15)), r=[utn, "h2T"], w=[pn])
                                A(lambda e, p=p, cc=cc, tt=tt: e.activation(out=AT[:, cc, tt * TT:(tt + 1) * TT], in_=p[:, 0:TT],
                                                                            func=AF.Gelu_apprx_tanh), r=[pn], w=["AT"])
                        LAG = 3
                        pgs = {}

                        def issue_gt(i, h, gbf, gbn):
                            if h == 0:
                                pgs[i] = psG.next()
                            pg, pgn = pgs[i]
                            for cc in range(GCH):
                                T(lambda e, pg=pg, gbf=gbf, cc=cc, h=h: e.matmul(
                                    pg[:, cc * 128:(cc + 1) * 128], lhsT=gbf[:, cc * 128:(cc + 1) * 128], rhs=ident_b[:],
                                    start=(h == 0 and cc == 0), stop=(h == 7 and cc == GCH - 1), skip_group_check=True),
                                  r=[gbn, "ident_b"], w=[pgn])
                            if h == 7:
                                V(lambda e, pg=pg, i=i: e.tensor_tensor(out=AT[:, :, i * 128:(i + 1) * 128],
                                                                        in0=pg[:].rearrange("p (c t) -> p c t", t=128),
                                                                        in1=AT[:, :, i * 128:(i + 1) * 128], op=ALU.mult),
                                  r=[pgn, "AT"], w=["AT"])

                        pend = []
                        for i in range(NTILE):
                            for h in range(8):
                                idx = i * 8 + h
                                pd, pdn = psD.next()
                                eb, ebn = Ebuf.next()
                                gbf, gbn = Gbuf.next()
                                T(lambda e, pd=pd, h=h, i=i, k1=k1: e.matmul(pd[:], lhsT=qT[:, 2 * h, i * 128:(i + 1) * 128],
                                                                             rhs=k1[:].rearrange("p a b -> p (a b)"), start=True, stop=False),
                                  r=["qT", k1n], w=[pdn])
                                T(lambda e, pd=pd, h=h, i=i: e.matmul(pd[:], lhsT=qT[:, 2 * h + 1, i * 128:(i + 1) * 128],
                                                                      rhs=Kx2[:].rearrange("p a b -> p (a b)"), start=False, stop=True),
                                  r=["qT", "Kx2"], w=[pdn])
                                A(lambda e, pd=pd, eb=eb, idx=idx: e.activation(out=eb[:], in_=pd[:], func=AF.Exp, bias=thb[:, 1, idx:idx + 1]),
                                  r=[pdn, "thb"], w=[ebn])
                                V(lambda e, pd=pd, eb=eb, gbf=gbf, idx=idx: e.scalar_tensor_tensor(
                                    out=gbf[:], in0=pd[:], scalar=thb[:, 0, idx:idx + 1], in1=eb[:], op0=ALU.is_ge, op1=ALU.mult),
                                  r=[pdn, "thb", ebn], w=[gbn])
                                pend.append((i, h, gbf, gbn))
                                if len(pend) > LAG:
                                    issue_gt(*pend.pop(0))
                        while pend:
                            issue_gt(*pend.pop(0))
                        for dc in range(16):
                            for tt in range(3):
                                po, pon = psO.next()
                                for cc in range(GCH):
                                    T(lambda e, po=po, cc=cc, dc=dc, tt=tt: e.matmul(
                                        po[:, 0:TT], lhsT=Vb[:, cc, dc * 128:(dc + 1) * 128], rhs=AT[:, cc, tt * TT:(tt + 1) * TT],
                                        start=(cc == 0), stop=(cc == GCH - 1)), r=["Vb%d" % cc, "AT"], w=[pon])
                                if grp == 0:
                                    V(lambda e, po=po, dc=dc, tt=tt: e.tensor_copy(out=PT[:, dc, tt * TT:(tt + 1) * TT], in_=po[:, 0:TT]),
                                      r=[pon], w=["PT"])
                                else:
                                    V(lambda e, po=po, dc=dc, tt=tt: e.tensor_tensor(out=PT[:, dc, tt * TT:(tt + 1) * TT], in0=po[:, 0:TT],
                                                                                     in1=PT[:, dc, tt * TT:(tt + 1) * TT], op=ALU.add),
                                      r=[pon, "PT"], w=["PT"])
                    if dbg:
                        DMA(dbg_PT, PT[:], "dbg_PT", r=["PT"])
                    S.barrier()
                    lst.__exit__(None, None, None)
                    fst = ExitStack()
                    with fst:
                        def fsb(name, shape, dt):
                            return fst.enter_context(nc.sbuf_tensor(uq(name), list(shape), dt))
                        g2 = [fsb("g2_%d" % g, [128, DM], F32) for g in range(2)]
                        fgb = fsb("fgb", [128, DM], F32)
                        xf_pool = Rot([(fsb("xf%d" % i, [128, DM], F32), "xf%d" % i) for i in range(2)])
                        tmpf = Rot([(fsb("tmpf%d" % i, [128, 512], F32), "tmpf%d" % i) for i in range(2)])
                        junkf = fsb("junkf", [128, DM], BF16)
                        for g in range(2):
                            DMA(g2[g][:], mods_d[3, g], "g2_%d" % g, r=["mods_d3"], w=["g2_%d" % g])
                        DMA(fgb[:], gvecs[2:3, :].to_broadcast([128, DM]), "fgb", w=["fgb"])
                        for i in range(NTILE):
                            g = 0 if i < 8 else 1
                            xf, xfn = xf_pool.next()
                            DMA(xf[:], x1_d[i * 128:(i + 1) * 128, :], xfn, r=["x1_d%d" % i], w=[xfn])
                            for db in range(4):
                                p, pn = pm.next()
                                for j in range(4):
                                    T(lambda e, p=p, j=j, db=db, i=i: e.transpose(out=p[:, j * 128:(j + 1) * 128],
                                                                                  in_=PT[:, db * 4 + j, i * 128:(i + 1) * 128],
                                                                                  identity=ident_f[:]),
                                      r=["PT", "ident_f"], w=[pn])
                                tf, tfn = tmpf.next()
                                V(lambda e, p=p, tf=tf, g=g, db=db: e.tensor_tensor(out=tf[:], in0=p[:], in1=g2[g][:, db * 512:(db + 1) * 512],
                                                                                    op=ALU.mult), r=[pn, "g2_%d" % g], w=[tfn])
                                V(lambda e, tf=tf, xf=xf, db=db: e.tensor_tensor(out=xf[:, db * 512:(db + 1) * 512], in0=xf[:, db * 512:(db + 1) * 512],
                                                                                 in1=tf[:], op=ALU.add), r=[tfn, xfn], w=[xfn])
                            si = stat_i[0] % 64
                            stat_i[0] += 1
                            sres = "stat%d" % si
                            A(lambda e, xf=xf, si=si: e.activation(out=junkf[:], in_=xf[:], func=AF.Square, accum_out=stats[:, si, 0:1]),
                              r=[xfn], w=["junkf", sres])
                            A(lambda e, si=si: e.activation(out=stats[:, si, 1:2], in_=stats[:, si, 0:1], func=AF.Sqrt, scale=1.0 / DM, bias=EPS),
                              r=[sres], w=[sres])
                            V(lambda e, si=si: e.reciprocal(out=stats[:, si, 2:3], in_=stats[:, si, 1:2]), r=[sres], w=[sres])
                            V(lambda e, xf=xf, si=si: e.scalar_tensor_tensor(out=xf[:], in0=xf[:], scalar=stats[:, si, 2:3], in1=fgb[:],
                                                                             op0=ALU.mult, op1=ALU.mult), r=[xfn, sres, "fgb"], w=[xfn])
                            DMA(y_d[i * 128:(i + 1) * 128, :], xf[:], xfn, r=[xfn], w=["y_out"])
        S.emit(gst)
    return nc


def make_in_maps(inp):
    f = lambda a: np.ascontiguousarray(np.asarray(a, dtype=np.float32))
    xp = f(inp["x_prompt"])
    xs = f(inp["x_sample"])
    cp = f(inp["c_prompt"])
    cs = f(inp["c_sample"])
    chvec = np.concatenate([f(inp["lru_conv_w"])[0], f(inp["lru_conv_b"]), f(inp["lru_ba"]), f(inp["lru_bx"]),
                            f(inp["lru_lambda"]), f(inp["sconv_w"])[0], f(inp["gnorm_lru_g"]), f(inp["gnorm_sc_g"])], axis=0)
    gvecs = np.stack([f(inp["norm1_g"])[0], f(inp["norm2_g"])[0], f(inp["final_g"])], axis=0)
    shared = {
        "w_ada": f(inp["w_ada"])[0], "b_ada": f(inp["b_ada"]), "gvecs": f(gvecs), "w_in": f(inp["w_in"])[0],
        "chvec": f(chvec), "lru_wa": f(inp["lru_wa"])[0], "lru_wx": f(inp["lru_wx"])[0], "w_out": f(inp["w_out"])[0],
        "wq": f(inp["peer_wq"])[0], "keys": f(inp["peer_sub_keys"])[0].reshape(256, 128),
        "puT": np.ascontiguousarray(f(inp["peer_u"])[0].T), "pv": f(inp["peer_v"])[0],
    }
    maps = []
    for k in range(8):
        b, hh = k // 2, k % 2
        sl = slice(16 * k, 16 * (k + 1))
        m = dict(shared)
        m["xm"] = f(np.concatenate([xp[b, hh * 1024:(hh + 1) * 1024], xs[sl].reshape(128, DM)], axis=0))
        m["xpre"] = f(xp[b, 0:1024])
        m["ctok"] = f(np.concatenate([np.repeat(cp[b:b + 1], 128, axis=0), np.repeat(cs[sl], 8, axis=0)], axis=0))
        m["flag"] = np.full((128, 1), float(hh), np.float32)
        m["st_h"] = f(inp["state_lru_h"][0][sl])
        m["st_conv"] = f(inp["state_lru_conv"][0][sl].reshape(48, 1024))
        m["st_sc"] = f(inp["state_sconv"][0][sl].reshape(32, 1024))
        maps.append(m)
    return maps


_NC_CACHE = {}


def kernel(**inputs):
    if "nc" not in _NC_CACHE:
        _NC_CACHE["nc"] = build_program()
    nc = _NC_CACHE["nc"]
    maps = make_in_maps(inputs)
    res = run_bass_kernel_spmd(nc, maps, core_ids=list(range(8)))
    R = res.results
    y_p = np.zeros((4, 2048, DM), np.float32)
    y_s = np.zeros((128, 8, DM), np.float32)
    hp = np.zeros((1, 4, 1024), np.float32)
    cvp = np.zeros((1, 4, 3, 1024), np.float32)
    scp = np.zeros((1, 4, 2, 1024), np.float32)
    hs = np.zeros((1, 128, 1024), np.float32)
    cvs = np.zeros((1, 128, 3, 1024), np.float32)
    scs = np.zeros((1, 128, 2, 1024), np.float32)
    for k in range(8):
        b, hh = k // 2, k % 2
        r = R[k]
        y_p[b, hh * 1024:(hh + 1) * 1024] = r["y"][0:1024]
        y_s[16 * k:16 * (k + 1)] = r["y"][1024:].reshape(16, 8, DM)
        if hh == 1:
            hp[0, b] = r["hl_p"].reshape(1024)
            cvp[0, b] = r["cv_p"]
            scp[0, b] = r["sc_p"]
        hs[0, 16 * k:16 * (k + 1)] = r["hl_s"]
        cvs[0, 16 * k:16 * (k + 1)] = r["cv_s"].reshape(16, 3, 1024)
        scs[0, 16 * k:16 * (k + 1)] = r["sc_s"].reshape(16, 2, 1024)
    return (y_p, y_s, hp, cvp, scp, hs, cvs, scs)
```
